# Optimizing a Trainium2 kernel written in Bass

```python
import jax, jax.numpy as jnp
from jax import lax
import numpy as np

D_MODEL = 1024
BATCH = 8
SEQ = 2048
DEPTH = 4

D_MIX = D_MODEL
SSD_HEAD_DIM = 64
SSD_WIDTH = D_MIX // 2
SSD_HEADS = SSD_WIDTH // SSD_HEAD_DIM
SSD_GROUPS = 2
SSD_HEADS_PER_GROUP = SSD_HEADS // SSD_GROUPS
D_STATE = 128
CONV_WIDTH = 4
CHUNK = 128
CONV_DIM = SSD_WIDTH + 2 * SSD_GROUPS * D_STATE
SB_HEAD_DIM = 64
SB_WIDTH = D_MIX // 4
SB_HEADS = SB_WIDTH // SB_HEAD_DIM
Q_BLOCK = 128
POOL_WINDOWS = (2, 4, 8, 16)
POOL_GROUPS = len(POOL_WINDOWS)
POOL_WIDTH = D_MIX - SSD_WIDTH - SB_WIDTH
POOL_GROUP_DIM = POOL_WIDTH // POOL_GROUPS
D_IN_PROJ = SSD_WIDTH + CONV_DIM + SSD_HEADS + 3 * SB_WIDTH + POOL_WIDTH
SPLIT_POINTS = (SSD_WIDTH,
                SSD_WIDTH + CONV_DIM,
                SSD_WIDTH + CONV_DIM + SSD_HEADS,
                SSD_WIDTH + CONV_DIM + SSD_HEADS + 3 * SB_WIDTH)
D_FF = -(-8 * D_MODEL // (3 * 256)) * 256
EPS = 1e-6

kernel_name = 'hybrid_ssd_stickbreak_pool_trunk'


def rmsnorm(x, w):
    xf = x.astype(jnp.float32)
    y = xf * lax.rsqrt(jnp.mean(xf * xf, axis=-1, keepdims=True) + EPS)
    return (y * w.astype(jnp.float32)).astype(x.dtype)


def causal_depthwise_conv(u, w, b):
    out = lax.conv_general_dilated(
        u, w[:, None, :].astype(u.dtype), window_strides=(1,),
        padding=[(CONV_WIDTH - 1, 0)],
        dimension_numbers=('NWC', 'WIO', 'NWC'),
        feature_group_count=u.shape[-1])
    return out + b.astype(u.dtype)


def ssd_mixer(z, xbc, dt_raw, conv_w, conv_b, dt_bias, a_log, d_skip, norm_w):
    f32 = jnp.float32
    bsz, seqlen, _ = xbc.shape
    nc = seqlen // CHUNK
    G, K, P, N, L = SSD_GROUPS, SSD_HEADS_PER_GROUP, SSD_HEAD_DIM, D_STATE, CHUNK
    xbc = jax.nn.silu(causal_depthwise_conv(xbc, conv_w, conv_b))
    xs, bm, cm = jnp.split(xbc, [SSD_WIDTH, SSD_WIDTH + SSD_GROUPS * D_STATE], axis=-1)
    dt = jax.nn.softplus(dt_raw.astype(f32) + dt_bias.astype(f32))
    a = -jnp.exp(a_log.astype(f32))
    xh = xs.astype(f32).reshape(bsz, nc, L, G, K, P)
    dtc = dt.reshape(bsz, nc, L, G, K)
    X = xh * dtc[..., None]
    Bc = bm.astype(f32).reshape(bsz, nc, L, G, N)
    Cc = cm.astype(f32).reshape(bsz, nc, L, G, N)
    dA = jnp.transpose(dtc * a.reshape(G, K), (0, 1, 3, 4, 2))
    acum = jnp.cumsum(dA, axis=-1)
    causal = jnp.tril(jnp.ones((L, L), dtype=bool))
    seg = jnp.where(causal, acum[..., :, None] - acum[..., None, :], -jnp.inf)
    decay_in = jnp.exp(seg)
    cb = jnp.einsum('bclgn,bcsgn->bcgls', Cc, Bc)
    y_diag = jnp.einsum('bcgls,bcgkls,bcsgkp->bclgkp', cb, decay_in, X)
    decay_to_end = jnp.exp(acum[..., -1:] - acum)
    chunk_states = jnp.einsum('bclgn,bcgkl,bclgkp->bcgkpn', Bc, decay_to_end, X)
    chunk_decay = jnp.exp(acum[..., -1])

    def step(state, inp):
        st, dec = inp
        return state * dec[..., None, None] + st, state

    init = jnp.zeros((bsz, G, K, P, N), f32)
    _, prev = lax.scan(step, init, (jnp.moveaxis(chunk_states, 1, 0), jnp.moveaxis(chunk_decay, 1, 0)))
    prev = jnp.moveaxis(prev, 0, 1)
    y_off = jnp.einsum('bclgn,bcgkpn,bcgkl->bclgkp', Cc, prev, jnp.exp(acum))
    y = y_diag + y_off + xh * d_skip.astype(f32).reshape(G, K)[:, :, None]
    y = y.reshape(bsz, seqlen, SSD_WIDTH) * jax.nn.silu(z.astype(f32))
    return rmsnorm(y, norm_w).astype(z.dtype)


def stick_breaking_attention(q, k, v):
    f32 = jnp.float32
    bsz, seqlen, _ = q.shape
    qh = q.astype(f32).reshape(bsz, seqlen, SB_HEADS, SB_HEAD_DIM) * (SB_HEAD_DIM ** -0.5)
    kh = k.astype(f32).reshape(bsz, seqlen, SB_HEADS, SB_HEAD_DIM)
    vh = v.astype(f32).reshape(bsz, seqlen, SB_HEADS, SB_HEAD_DIM)
    outs = []
    for start in range(0, seqlen, Q_BLOCK):
        end = start + Q_BLOCK
        logits = jnp.einsum('bthd,bshd->bhts', qh[:, start:end], kh[:, :end])
        before = jnp.arange(end)[None, :] < jnp.arange(start, end)[:, None]
        log_keep = jnp.where(before, jax.nn.log_sigmoid(-logits), 0.0)
        log_keep_after = lax.cumsum(log_keep, axis=3, reverse=True) - log_keep
        w = jnp.where(before, jnp.exp(jax.nn.log_sigmoid(logits) + log_keep_after), 0.0)
        outs.append(jnp.einsum('bhts,bshd->bthd', w, vh[:, :end]))
    o = jnp.concatenate(outs, axis=1)
    return o.reshape(bsz, seqlen, SB_WIDTH).astype(q.dtype)


def multiscale_pool(p, pool_w, pool_b, pool_scale):
    f32 = jnp.float32
    bsz, seqlen, _ = p.shape
    groups = p.astype(f32).reshape(bsz, seqlen, POOL_GROUPS, POOL_GROUP_DIM)
    csum = jnp.pad(jnp.cumsum(groups, axis=1), ((0, 0), (1, 0), (0, 0), (0, 0)))
    pos = jnp.arange(seqlen)
    pooled = []
    for gi, win in enumerate(POOL_WINDOWS):
        cg = csum[:, :, gi]
        lo = jnp.maximum(pos + 1 - win, 0)
        wsum = cg[:, 1:] - cg[:, lo]
        count = jnp.minimum(pos + 1, win).astype(f32)
        pooled.append(wsum / count[None, :, None] - groups[:, :, gi])
    pooled = jnp.stack(pooled, axis=2)
    mixed = jnp.einsum('bsgc,gcd->bsgd', pooled, pool_w.astype(f32)) + pool_b.astype(f32)
    return (mixed.reshape(bsz, seqlen, POOL_WIDTH) * pool_scale.astype(f32)).astype(p.dtype)


def setup_inputs(seed: int = 0) -> dict:
    key = jax.random.key(seed)
    ks = jax.random.split(key, 20)
    f32 = jnp.float32
    nrm = lambda k, shape, s: jax.random.normal(k, shape, f32) * s
    dt0 = jnp.exp(jax.random.uniform(ks[5], (DEPTH, SSD_HEADS), f32, np.log(1e-3), np.log(1e-1)))
    return {
        'x': jax.random.normal(ks[0], (BATCH, SEQ, D_MODEL), f32),
        'norm1_w': 1.0 + nrm(ks[1], (DEPTH, D_MODEL), 0.02),
        'w_in': nrm(ks[2], (DEPTH, D_MODEL, D_IN_PROJ), D_MODEL ** -0.5),
        'conv_w': nrm(ks[3], (DEPTH, CONV_WIDTH, CONV_DIM), CONV_WIDTH ** -0.5),
        'conv_b': nrm(ks[4], (DEPTH, CONV_DIM), 0.02),
        'dt_bias': dt0 + jnp.log(-jnp.expm1(-dt0)),
        'a_log': jnp.log(jax.random.uniform(ks[6], (DEPTH, SSD_HEADS), f32, 1.0, 16.0)),
        'd_skip': 1.0 + nrm(ks[7], (DEPTH, SSD_HEADS), 0.1),
        'ssd_norm_w': 1.0 + nrm(ks[8], (DEPTH, SSD_WIDTH), 0.02),
        'pool_w': nrm(ks[9], (DEPTH, POOL_GROUPS, POOL_GROUP_DIM, POOL_GROUP_DIM), POOL_GROUP_DIM ** -0.5),
        'pool_b': nrm(ks[10], (DEPTH, POOL_GROUPS, POOL_GROUP_DIM), 0.02),
        'pool_scale': 1.0 + nrm(ks[11], (DEPTH, POOL_WIDTH), 0.1),
        'w_out': nrm(ks[12], (DEPTH, D_MIX, D_MODEL), D_MIX ** -0.5),
        'norm2_w': 1.0 + nrm(ks[13], (DEPTH, D_MODEL), 0.02),
        'w_gate': nrm(ks[14], (DEPTH, D_MODEL, D_FF), D_MODEL ** -0.5),
        'w_up': nrm(ks[15], (DEPTH, D_MODEL, D_FF), D_MODEL ** -0.5),
        'w_down': nrm(ks[16], (DEPTH, D_FF, D_MODEL), D_FF ** -0.5),
        'final_norm_w': 1.0 + nrm(ks[17], (D_MODEL,), 0.02),
    }


def reference(x, norm1_w, w_in, conv_w, conv_b, dt_bias, a_log, d_skip, ssd_norm_w,
              pool_w, pool_b, pool_scale, w_out, norm2_w, w_gate, w_up, w_down, final_norm_w):
    for layer in range(DEPTH):
        h = rmsnorm(x, norm1_w[layer])
        proj = h @ w_in[layer]
        z, xbc, dt_raw, qkv, p = jnp.split(proj, list(SPLIT_POINTS), axis=-1)
        q, k, v = jnp.split(qkv, 3, axis=-1)
        y_ssd = ssd_mixer(z, xbc, dt_raw, conv_w[layer], conv_b[layer], dt_bias[layer],
                          a_log[layer], d_skip[layer], ssd_norm_w[layer])
        y_sb = stick_breaking_attention(q, k, v)
        y_pool = multiscale_pool(p, pool_w[layer], pool_b[layer], pool_scale[layer])
        y = jnp.concatenate([y_ssd, y_sb, y_pool], axis=-1)
        x = x + y @ w_out[layer]
        h = rmsnorm(x, norm2_w[layer])
        x = x + (jax.nn.silu(h @ w_gate[layer]) * (h @ w_up[layer])) @ w_down[layer]
    return rmsnorm(x, final_norm_w)
```

```python
import numpy as np
from contextlib import ExitStack
import concourse.bass as bass
import concourse.mybir as mybir
from concourse.bass_utils import run_bass_kernel_spmd

F32 = mybir.dt.float32
BF16 = mybir.dt.bfloat16
AF = mybir.ActivationFunctionType
ALU = mybir.AluOpType

D = 1024
T = 2048
DEPTH = 4
NCORES = 8
DFF = 2816
NFC = DFF // 128
TT = 512
NTT = T // TT
NBT = TT // 128
NB = T // 128
EPS = 1e-6
NPAR = 88

P_N1W, P_N2W, P_CW, P_CB, P_SNW, P_PB, P_PS, P_DTB, P_ALOG, P_DSK = 0, 8, 16, 48, 56, 60, 62, 64, 72, 80
CF_IDENT, CF_ONES, CF_TRI, CF_RCFIX, CF_N = 0, 128, 256, 384, 400
CB_IDENT, CB_ONES, CB_NEGTRI, CB_NEGONES, CB_MASK, CB_N = 0, 128, 256, 384, 512, 512 + 4 * 512


class Res:
    __slots__ = ("w", "r")

    def __init__(self):
        self.w = None
        self.r = {}


class Eng:
    def __init__(self, name, h, sem):
        self.name, self.h, self.sem = name, h, sem
        self.count = 0
        self.waited = {}


class KB:
    def __init__(self, nc, stack, n_dma_sems=40):
        self.nc = nc
        self.eng = {}
        for name, h in (("pe", nc.tensor), ("act", nc.scalar), ("dve", nc.vector), ("pool", nc.gpsimd), ("sp", nc.sync)):
            sem = stack.enter_context(nc.semaphore("sem_" + name))
            self.eng[name] = Eng(name, h, sem)
        self.dsem = [stack.enter_context(nc.semaphore("dsem%d" % i)) for i in range(n_dma_sems)]
        self.dval = [0] * n_dma_sems
        self.dnext = 0
        self.semkey = {}

    def _key(self, sem):
        return id(sem)

    def _deps(self, e, reads, writes):
        need = {}

        def add(tok, raw):
            sem, val = tok
            if sem is e.sem and (e.name == "pe" or not raw):
                return
            k = id(sem)
            if k not in need or need[k][1] < val:
                need[k] = (sem, val)

        for r in reads:
            if r.w is not None:
                add(r.w, True)
        for w in writes:
            if w.w is not None:
                add(w.w, False)
            for k, tok in w.r.items():
                add(tok, False)
        for k, (sem, val) in need.items():
            if e.waited.get(k, 0) < val:
                e.h.wait_ge(sem, val)
                e.waited[k] = val

    def _mark(self, tok, reads, writes):
        k = id(tok[0])
        for r in reads:
            r.r[k] = tok
        for w in writes:
            w.w = tok
            w.r = {}

    def op(self, en, fn, reads=(), writes=()):
        e = self.eng[en]
        self._deps(e, reads, writes)
        ins = fn(e.h)
        e.count += 1
        ins.then_inc(e.sem, 1)
        tok = (e.sem, e.count)
        self._mark(tok, reads, writes)
        return tok

    def dma(self, qn, out, in_, reads=(), writes=()):
        q = self.eng[qn]
        slot = self.dnext
        self.dnext = (slot + 1) % len(self.dsem)
        sem = self.dsem[slot]
        self._deps(q, reads, writes)
        if self.dval[slot] > 0 and q.waited.get(id(sem), 0) < self.dval[slot]:
            q.h.wait_ge(sem, self.dval[slot])
            q.waited[id(sem)] = self.dval[slot]
        ins = q.h.dma_start(out=out, in_=in_)
        self.dval[slot] += 16
        ins.then_inc(sem, 16)
        tok = (sem, self.dval[slot])
        self._mark(tok, reads, writes)
        return tok

    def barrier(self):
        toks = [(e.sem, e.count) for e in self.eng.values() if e.count > 0]
        for e in self.eng.values():
            for sem, val in toks:
                if sem is e.sem:
                    continue
                if e.waited.get(id(sem), 0) < val:
                    e.h.wait_ge(sem, val)
                    e.waited[id(sem)] = val

    def final_wait(self, en, reslist):
        e = self.eng[en]
        self._deps(e, reslist, [])


def build_program(depth=DEPTH, debug=False):
    nc = bass.Bass("TRN2", target_bir_lowering=False)
    dt_ = lambda n, s: nc.dram_tensor(n, s, F32, kind="ExternalInput").ap()
    xT_d = dt_("xT", [D, T])
    wfm_d = dt_("w_fm", [depth, 18, 128, 8 * 128])
    wtm_d = dt_("w_tm", [depth, 128, 8 * 264])
    wout_d = dt_("w_o", [depth, 8, 128, 8 * 128])
    wgu_d = dt_("w_gu", [depth, NFC, 2, 128, 8 * 128])
    wdn_d = dt_("w_dn", [depth, 8, 128, NFC * 128])
    par_d = dt_("par", [depth, 128, NPAR])
    pw_d = dt_("pw", [depth, 128, 2 * 128])
    fnw_d = dt_("fnw", [128, 8])
    cf_d = dt_("cf", [128, CF_N])
    cb_d = dt_("cb", [128, CB_N])
    out_d = nc.dram_tensor("outT", [D, T], F32, kind="ExternalOutput").ap()
    if debug:
        dby_d = nc.dram_tensor("dbg_y", [D, T], F32, kind="ExternalOutput").ap()
        dbx_d = nc.dram_tensor("dbg_x", [D, T], F32, kind="ExternalOutput").ap()

    with ExitStack() as st:
        K = KB(nc, st)
        uid = [0]

        def uname(name):
            uid[0] += 1
            return "s_%s_%d" % (name, uid[0])

        sb = lambda name, shape, dt: st.enter_context(nc.sbuf_tensor(uname(name), shape, dt))

        def R(n=None):
            return Res() if n is None else [Res() for _ in range(n)]

        xT = sb("xT", [128, 8, T], F32)
        xT_r = [[Res() for _ in range(NTT)] for _ in range(8)]
        kT = sb("kT", [128, 2, T], BF16)
        kT_r = [[Res() for _ in range(NTT)] for _ in range(2)]
        vtok = sb("vtok", [128, NB, 256], BF16)
        vtok_r = R(NB)
        hT = sb("hT", [128, 8, TT], BF16)
        hT_r = R(8)
        zs = sb("zs", [128, 4, TT], BF16)
        zs_r = R(4)
        ysb = sb("ysb", [128, 2, TT], BF16)
        ysb_r = R(2)
        ypl = sb("ypl", [128, 2, TT], BF16)
        ypl_r = R(2)
        xstok = sb("xstok", [128, NBT, 512], BF16)
        xstok_r = R(NBT)
        BTt = sb("BT", [128, 2, TT], BF16)
        BT_r = R(2)
        Btok = sb("Btok", [128, NBT, 256], BF16)
        Btok_r = R(NBT)
        CTt = sb("CT", [128, 2, TT], BF16)
        CT_r = R(2)
        qT = sb("qT", [128, 2, TT], BF16)
        qT_r = R(2)
        cf = sb("cf", [128, CF_N], F32)
        cf_r = R()
        cb = sb("cb", [128, CB_N], BF16)
        cb_r = R()
        par = [sb("par%d" % i, [128, NPAR], F32) for i in range(2)]
        par_r = R(2)
        fnw = sb("fnw", [128, 8], F32)
        fnw_r = R()
        DI = sb("DI", [128, 8, 128], BF16)
        DI_r = R()
        pwbd = sb("pwbd", [128, 2, 128], BF16)
        pwbd_r = R()
        arep = sb("arep", [128, 8], F32)
        arep_r = R()
        prev32 = sb("prev32", [128, 8, 64], F32)
        prev32_r = R()
        prevbf = sb("prevbf", [128, 8, 64], BF16)
        prevbf_r = R()
        chalo = sb("chalo", [128, 8, 4], F32)
        chalo_r = R(8)
        pst = sb("pst", [128, 2, 16 + TT], F32)
        pst_r = R(2)
        smalls = {}
        for nm in ("dtr", "ee", "dtt", "dA", "acum", "tot", "cd", "dd", "dte", "xsc"):
            smalls[nm] = (sb("sm_" + nm, [128, NBT, 8], F32), Res())
        NA, NTM, ND = 6, 1, 2
        wA = [sb("wA%d" % i, [128, 8, 128], BF16) for i in range(NA)]
        wA_r = R(NA)
        wTm = [sb("wT%d" % i, [128, 8, 264], BF16) for i in range(NTM)]
        wTm_r = R(NTM)
        wD = [sb("wD%d" % i, [128, NFC, 128], BF16) for i in range(ND)]
        wD_r = R(ND)
        psb = [st.enter_context(nc.psum_tensor("ps%d" % i, [128, 512], F32)) for i in range(8)]
        psb_r = R(8)
        psn = [0]

        held = set()

        def psum(hold=False):
            i = psn[0]
            while i in held:
                i = (i + 1) % 8
            psn[0] = (i + 1) % 8
            if hold:
                held.add(i)
            return psb[i], psb_r[i]

        def psum_release(ps):
            for i in range(8):
                if psb[i] is ps:
                    held.discard(i)

        ident_f = cf[:, CF_IDENT:CF_IDENT + 128]
        ones_f = cf[:, CF_ONES:CF_ONES + 128]
        tri_f = cf[:, CF_TRI:CF_TRI + 128]
        rcfix = cf[:, CF_RCFIX:CF_RCFIX + 16]
        ident_b = cb[:, CB_IDENT:CB_IDENT + 128]
        ones_b = cb[:, CB_ONES:CB_ONES + 128]
        negtri_b = cb[:, CB_NEGTRI:CB_NEGTRI + 128]
        negones_b = cb[:, CB_NEGONES:CB_NEGONES + 128]
        sbmask = [cb[:, CB_MASK + i * 512:CB_MASK + (i + 1) * 512] for i in range(4)]

        K.dma("sp", cf[:], cf_d[:], writes=[cf_r])
        K.dma("pool", cb[:], cb_d[:], writes=[cb_r])
        K.dma("sp", fnw[:], fnw_d[:], writes=[fnw_r])
        xT_v = xT_d.rearrange("(c p) t -> p c t", p=128)
        for c in range(8):
            K.dma("sp", xT[:, c, :], xT_v[:, c, :], writes=xT_r[c])

        loadsA, loadsT, loadsD = [], [], []
        for l in range(depth):
            for tt in range(NTT):
                loadsT.append(wtm_d[l])
                for oc in range(18):
                    loadsA.append(wfm_d[l, oc])
                for oc in range(8):
                    loadsA.append(wout_d[l, oc])
                for fc in range(NFC):
                    loadsA.append(wgu_d[l, fc, 0])
                    loadsA.append(wgu_d[l, fc, 1])
                for oc in range(8):
                    loadsD.append(wdn_d[l, oc])

        class Stream:
            def __init__(self, loads, tiles, res, shp):
                self.loads, self.tiles, self.res, self.shp = loads, tiles, res, shp
                self.issued = 0
                self.used = 0

            def get(self):
                i = self.used
                n = len(self.tiles)
                while self.issued < len(self.loads) and self.issued <= i + n - self.shp:
                    j = self.issued
                    tl = self.tiles[j % n]
                    K.dma("pool", tl[:].rearrange("p a b -> p (a b)"), self.loads[j], writes=[self.res[j % n]])
                    self.issued += 1
                self.used += 1
                return self.tiles[i % n], self.res[i % n]

        SA = Stream(loadsA, wA, wA_r, 2)
        STm = Stream(loadsT, wTm, wTm_r, 1)
        SD = Stream(loadsD, wD, wD_r, 1)

        def mm(out, lhsT, rhs, start, stop, reads, writes):
            return K.op("pe", lambda h: h.matmul(out, lhsT, rhs, start=start, stop=stop), reads=reads, writes=writes)

        def act(out, in_, func, reads, writes, bias=None, scale=None):
            kw = {}
            if bias is not None:
                kw["bias"] = bias
            if scale is not None:
                kw["scale"] = scale
            return K.op("act", lambda h: h.activation(out=out, in_=in_, func=func, **kw), reads=reads, writes=writes)

        def vtt(out, in0, in1, op, reads, writes, en="dve"):
            return K.op(en, lambda h: h.tensor_tensor(out, in0, in1, op), reads=reads, writes=writes)

        def vts(out, in0, s1, s2, op0, op1, reads, writes, en="dve"):
            if s2 is None:
                return K.op(en, lambda h: h.tensor_scalar(out, in0, s1, None, op0), reads=reads, writes=writes)
            return K.op(en, lambda h: h.tensor_scalar(out, in0, s1, s2, op0, op1), reads=reads, writes=writes)

        def vstt(out, in0, sc, in1, op0, op1, reads, writes, en="dve"):
            return K.op(en, lambda h: h.scalar_tensor_tensor(out, in0, sc, in1, op0, op1), reads=reads, writes=writes)

        def vcopy(out, in_, reads, writes, en="dve"):
            return K.op(en, lambda h: h.tensor_copy(out, in_), reads=reads, writes=writes)

        def vmemset(ap, val, writes, en="dve"):
            return K.op(en, lambda h: h.memset(ap, val), writes=writes)

        def bc_mid(ap2, n):
            return ap2.unsqueeze(1).broadcast_to([ap2.shape[0], n, ap2.shape[1]])

        def bc_last(ap2, n):
            return ap2.unsqueeze(2).broadcast_to([ap2.shape[0], ap2.shape[1], n])

        dbg_outs = {}
        out_res = [Res() for _ in range(8 * NTT)]
        dbg_res = Res()

        def rmsnorm_tile(src_chunks, src_res, nchunk, wcols, wres, dst_fn, dst_res, nfeat, tmp):
            sqb, sqb_r, lnv, lnv_r, rstd, rstd_r = tmp
            for c in range(nchunk):
                act(sqb[:, c, :], src_chunks[c], AF.Square, reads=[src_res[c]], writes=[sqb_r[c]])
            ps, ps_r = psum()
            for c in range(nchunk):
                mm(ps[:, :], ones_b, sqb[:, c, :], c == 0, c == nchunk - 1, reads=[cb_r, sqb_r[c]], writes=[ps_r])
            act(lnv[:, :], ps[:, :], AF.Ln, reads=[ps_r], writes=[lnv_r], bias=EPS, scale=1.0 / nfeat)
            act(rstd[:, :], lnv[:, :], AF.Exp, reads=[lnv_r], writes=[rstd_r], scale=-0.5)
            for c in range(nchunk):
                vstt(dst_fn(c), src_chunks[c], wcols[c], rstd[:, :], ALU.mult, ALU.mult,
                     reads=[src_res[c], wres, rstd_r], writes=[dst_res[c]])

        for l in range(depth):
            pr = par[l % 2]
            pr_r = par_r[l % 2]
            K.dma("sp", pr[:], par_d[l], writes=[pr_r])
            K.dma("pool", pwbd[:].rearrange("p a b -> p (a b)"), pw_d[l], writes=[pwbd_r])
            vmemset(prev32[:], 0.0, [prev32_r])
            vmemset(prevbf[:], 0.0, [prevbf_r])
            vmemset(chalo[:], 0.0, chalo_r)
            vmemset(pst[:], 0.0, pst_r)
            act(arep[:, :], pr[:, P_ALOG:P_ALOG + 8], AF.Exp, reads=[pr_r], writes=[arep_r])
            vts(arep[:, :], arep[:, :], -1.0, None, ALU.mult, None, reads=[arep_r], writes=[arep_r])
            for h in range(8):
                vts(DI[:, h, :], ident_f, pr[:, P_DSK + h:P_DSK + h + 1], None, ALU.mult, None,
                    reads=[cf_r, pr_r], writes=[DI_r])

            for tt in range(NTT):
                tsl = slice(tt * TT, (tt + 1) * TT)
                with ExitStack() as ph:
                    tb = lambda name, shape, dt: ph.enter_context(nc.sbuf_tensor(uname(name), shape, dt))
                    sqb = tb("sqb", [128, 8, TT], BF16); sqb_r = R(8)
                    lnv = tb("lnv", [128, TT], F32); lnv_r = R()
                    rstd = tb("rstd", [128, TT], F32); rstd_r = R()
                    cst = [tb("cst%d" % i, [128, 4 + TT], F32) for i in range(2)]; cst_r = R(2)
                    ctmp = [tb("ctmp%d" % i, [128, TT], F32) for i in range(2)]; ctmp_r = R(2)
                    cact = [tb("cact%d" % i, [128, TT], BF16) for i in range(2)]; cact_r = R(2)
                    s2 = tb("s2", [128, 16 + TT], F32); s2_r = R()
                    s4 = tb("s4", [128, 16 + TT], F32); s4_r = R()
                    s8 = tb("s8", [128, 16 + TT], F32); s8_r = R()
                    s16 = tb("s16", [128, 16 + TT], F32); s16_r = R()
                    pfix = tb("pfix", [128, 16], F32); pfix_r = R()
                    pooled = [tb("pooled%d" % i, [128, TT], BF16) for i in range(2)]; pooled_r = R(2)
                    dtr, dtr_r = smalls["dtr"]

                    rmsnorm_tile([xT[:, c, tsl] for c in range(8)], [xT_r[c][tt] for c in range(8)], 8,
                                 [pr[:, P_N1W + c:P_N1W + c + 1] for c in range(8)], pr_r,
                                 lambda c: hT[:, c, :], hT_r, D, (sqb, sqb_r, lnv, lnv_r, rstd, rstd_r))

                    wt, wt_r = STm.get()
                    for blk in range(NBT):
                        ps, ps_r = psum()
                        bsl = slice(blk * 128, (blk + 1) * 128)
                        for kc in range(8):
                            mm(ps[:, 0:264], hT[:, kc, bsl], wt[:, kc, :], kc == 0, kc == 7,
                               reads=[hT_r[kc], wt_r], writes=[ps_r])
                        gb = tt * NBT + blk
                        act(vtok[:, gb, :], ps[:, 0:256], AF.Copy, reads=[ps_r], writes=[vtok_r[gb]])
                        vtt(dtr[:, blk, :], ps[:, 256:264], pr[:, P_DTB:P_DTB + 8], ALU.add,
                            reads=[ps_r, pr_r], writes=[dtr_r])

                    for oc in range(18):
                        wa, wa_r = SA.get()
                        ps, ps_r = psum()
                        for kc in range(8):
                            mm(ps[:, :], wa[:, kc, :], hT[:, kc, :], kc == 0, kc == 7,
                               reads=[wa_r, hT_r[kc]], writes=[ps_r])
                        if oc < 4:
                            act(zs[:, oc, :], ps[:, :], AF.Silu, reads=[ps_r], writes=[zs_r[oc]])
                        elif oc < 12:
                            j = oc - 4
                            sl_ = j % 2
                            cs, cs_r = cst[sl_], cst_r[sl_]
                            ct, ct_r = ctmp[sl_], ctmp_r[sl_]
                            ca, ca_r = cact[sl_], cact_r[sl_]
                            vcopy(cs[:, 0:3], chalo[:, j, 0:3], reads=[chalo_r[j]], writes=[cs_r])
                            act(cs[:, 3:3 + TT], ps[:, :], AF.Copy, reads=[ps_r], writes=[cs_r])
                            vcopy(chalo[:, j, 0:3], cs[:, TT:TT + 3], reads=[cs_r], writes=[chalo_r[j]])
                            cw = lambda i: pr[:, P_CW + j * 4 + i:P_CW + j * 4 + i + 1]
                            vts(ct[:, :], cs[:, 0:TT], cw(0), pr[:, P_CB + j:P_CB + j + 1], ALU.mult, ALU.add,
                                reads=[cs_r, pr_r], writes=[ct_r])
                            for i in range(1, 4):
                                vstt(ct[:, :], cs[:, i:i + TT], cw(i), ct[:, :], ALU.mult, ALU.add,
                                     reads=[cs_r, pr_r, ct_r], writes=[ct_r])
                            if j < 4:
                                act(ca[:, :], ct[:, :], AF.Silu, reads=[ct_r], writes=[ca_r])
                                src, src_r = ca, ca_r
                            elif j < 6:
                                g = j - 4
                                act(BTt[:, g, :], ct[:, :], AF.Silu, reads=[ct_r], writes=[BT_r[g]])
                                src, src_r = None, BT_r[g]
                            else:
                                g = j - 6
                                act(CTt[:, g, :], ct[:, :], AF.Silu, reads=[ct_r], writes=[CT_r[g]])
                            if j < 6:
                                pt, pt_r = psum()
                                ptb = pt[:, :].bitcast(BF16)
                                for blk in range(NBT):
                                    bsl = slice(blk * 128, (blk + 1) * 128)
                                    srcap = ca[:, bsl] if j < 4 else BTt[:, j - 4, bsl]
                                    K.op("pe", lambda h, o=ptb[:, blk * 128:(blk + 1) * 128], s=srcap: h.transpose(o, s, ident_b),
                                         reads=[src_r, cb_r], writes=[pt_r])
                                if j < 4:
                                    vcopy(xstok[:, :, j * 128:(j + 1) * 128],
                                          ptb[:, 0:NBT * 128].rearrange("p (b c) -> p b c", b=NBT),
                                          reads=[pt_r], writes=xstok_r)
                                else:
                                    g = j - 4
                                    vcopy(Btok[:, :, g * 128:(g + 1) * 128],
                                          ptb[:, 0:NBT * 128].rearrange("p (b c) -> p b c", b=NBT),
                                          reads=[pt_r], writes=Btok_r)
                        elif oc < 14:
                            i = oc - 12
                            K.op('act', lambda h, o=qT[:, i, :], s=ps[:, :]: h.mul(o, s, 0.125), reads=[ps_r], writes=[qT_r[i]])
                        elif oc < 16:
                            i = oc - 14
                            act(kT[:, i, tsl], ps[:, :], AF.Copy, reads=[ps_r], writes=[kT_r[i][tt]])
                        else:
                            i = oc - 16
                            A = pst[:, i, :]
                            A_r = pst_r[i]
                            W = 16 + TT
                            act(A[:, 16:W], ps[:, :], AF.Copy, reads=[ps_r], writes=[A_r])
                            vtt(s2[:, 1:W], A[:, 1:W], A[:, 0:W - 1], ALU.add, reads=[A_r], writes=[s2_r])
                            vtt(s4[:, 3:W], s2[:, 3:W], s2[:, 1:W - 2], ALU.add, reads=[s2_r], writes=[s4_r])
                            if i == 0:
                                lev = [(s2, s2_r, 2), (s4, s4_r, 4)]
                            else:
                                vtt(s8[:, 7:W], s4[:, 7:W], s4[:, 3:W - 4], ALU.add, reads=[s4_r], writes=[s8_r])
                                vtt(s16[:, 15:W], s8[:, 15:W], s8[:, 7:W - 8], ALU.add, reads=[s8_r], writes=[s16_r])
                                lev = [(s8, s8_r, 8), (s16, s16_r, 16)]
                            pl, pl_r = pooled[i], pooled_r[i]
                            for half, (sv, sv_r, win) in enumerate(lev):
                                psl = slice(half * 64, (half + 1) * 64)
                                vstt(pl[psl, :], sv[psl, 16:W], 1.0 / win, A[psl, 16:W], ALU.mult, ALU.subtract,
                                     reads=[sv_r, A_r], writes=[pl_r])
                                if tt == 0 and win > 1:
                                    nfx = win - 1
                                    vtt(pfix[psl, 0:nfx], sv[psl, 16:16 + nfx], rcfix[psl, 0:nfx], ALU.mult,
                                        reads=[sv_r, cf_r], writes=[pfix_r])
                                    vtt(pl[psl, 0:nfx], pfix[psl, 0:nfx], A[psl, 16:16 + nfx], ALU.subtract,
                                        reads=[pfix_r, A_r], writes=[pl_r])
                            vcopy(A[:, 0:16], A[:, TT:TT + 16], reads=[A_r], writes=[A_r])
                            ps2, ps2_r = psum()
                            mm(ps2[:, :], pwbd[:, i, :], pl[:, :], True, True, reads=[pwbd_r, pl_r], writes=[ps2_r])
                            vts(ypl[:, i, :], ps2[:, :], pr[:, P_PB + i:P_PB + i + 1], pr[:, P_PS + i:P_PS + i + 1],
                                ALU.add, ALU.mult, reads=[ps2_r, pr_r], writes=[ypl_r[i]])


                    ee, ee_r = smalls["ee"]
                    dtt, dtt_r = smalls["dtt"]
                    dA, dA_r = smalls["dA"]
                    acum, acum_r = smalls["acum"]
                    tot, tot_r = smalls["tot"]
                    cd, cd_r = smalls["cd"]
                    dd, dd_r = smalls["dd"]
                    dte, dte_r = smalls["dte"]
                    xsc, xsc_r = smalls["xsc"]
                    act(ee[:], dtr[:], AF.Exp, reads=[dtr_r], writes=[ee_r])
                    act(dtt[:], ee[:], AF.Ln, reads=[ee_r], writes=[dtt_r], bias=1.0)
                    vtt(dA[:], dtt[:], bc_mid(arep[:, :], NBT), ALU.mult, reads=[dtt_r, arep_r], writes=[dA_r])
                    ps, ps_r = psum()
                    dA2 = dA[:].rearrange("p b h -> p (b h)")
                    mm(ps[:, 0:NBT * 8], tri_f, dA2, True, True, reads=[cf_r, dA_r], writes=[ps_r])
                    mm(ps[:, 64:64 + NBT * 8], ones_f, dA2, True, True, reads=[cf_r, dA_r], writes=[ps_r])
                    vcopy(acum[:].rearrange("p b h -> p (b h)"), ps[:, 0:NBT * 8], reads=[ps_r], writes=[acum_r])
                    vcopy(tot[:].rearrange("p b h -> p (b h)"), ps[:, 64:64 + NBT * 8], reads=[ps_r], writes=[tot_r])
                    act(cd[:], tot[:], AF.Exp, reads=[tot_r], writes=[cd_r])
                    vtt(dd[:], tot[:], acum[:], ALU.subtract, reads=[tot_r, acum_r], writes=[dd_r])
                    act(dte[:], dd[:], AF.Exp, reads=[dd_r], writes=[dte_r])
                    vtt(xsc[:], dte[:], dtt[:], ALU.mult, reads=[dte_r, dtt_r], writes=[xsc_r])
                K.barrier()
                with ExitStack() as ph:
                    tb = lambda name, shape, dt: ph.enter_context(nc.sbuf_tensor(uname(name), shape, dt))
                    Rm = tb("Rm", [128, 8, 128], F32); Rm_r = R()
                    seg = [tb("seg%d" % i, [128, 4, 128], F32) for i in range(2)]; seg_r = R(2)
                    eex = [tb("eex%d" % i, [128, 4, 128], F32) for i in range(2)]; eex_r = R(2)
                    Mm = [tb("Mm%d" % i, [128, 4, 128], BF16) for i in range(2)]; Mm_r = R(2)
                    Eh = [tb("Eh%d" % i, [128, 4, 128], BF16) for i in range(2)]; Eh_r = R(2)
                    Cs = [tb("Cs%d" % i, [128, 4, 128], BF16) for i in range(2)]; Cs_r = R(2)
                    cbm = tb("cbm", [128, 2, 128], F32); cbm_r = R()
                    Xs = tb("Xs", [128, 8, 64], BF16); Xs_r = R()
                    Xd = tb("Xd", [128, 8, 64], BF16); Xd_r = R()
                    ptmp = tb("ptmp", [128, 8, 64], F32); ptmp_r = R()
                    yg = tb("yg", [128, 4, TT], F32); yg_r = R(4)
                    sqb = tb("sqb2", [128, 4, TT], BF16); sqb_r = R(4)
                    lnv = tb("lnv2", [128, TT], F32); lnv_r = R()
                    rstd = tb("rstd2", [128, TT], F32); rstd_r = R()
                    for blk in range(NBT):
                        bsl = slice(blk * 128, (blk + 1) * 128)
                        vtt(Rm[:], bc_mid(tri_f, 8), bc_last(dA[:, blk, :], 128), ALU.mult,
                            reads=[cf_r, dA_r], writes=[Rm_r])
                        psA, psA_r = psum()
                        psB, psB_r = psum()
                        mm(psA[:, :], ones_f, Rm[:, 0:4, :].rearrange("p a b -> p (a b)"), True, True,
                           reads=[cf_r, Rm_r], writes=[psA_r])
                        mm(psB[:, :], ones_f, Rm[:, 4:8, :].rearrange("p a b -> p (a b)"), True, True,
                           reads=[cf_r, Rm_r], writes=[psB_r])
                        psC, psC_r = psum()
                        for g in range(2):
                            mm(psC[:, g * 128:(g + 1) * 128], BTt[:, g, bsl], CTt[:, g, bsl], True, True,
                               reads=[BT_r[g], CT_r[g]], writes=[psC_r])
                        vtt(cbm[:], psC[:, 0:256].rearrange("p (g l) -> p g l", g=2), bc_mid(tri_f, 2), ALU.mult,
                            reads=[psC_r, cf_r], writes=[cbm_r])
                        for g in range(2):
                            pbc, pbc_r = (psA, psA_r) if g == 0 else (psB, psB_r)
                            pbc3 = pbc[:, :].rearrange("p (k l) -> p k l", k=4)
                            vtt(seg[g][:], pbc3, bc_last(acum[:, blk, 4 * g:4 * g + 4], 128), ALU.subtract,
                                reads=[pbc_r, acum_r], writes=[seg_r[g]])
                            act(eex[g][:], seg[g][:], AF.Exp, reads=[seg_r[g]], writes=[eex_r[g]])
                            vstt(Mm[g][:], eex[g][:], 1.0, bc_mid(cbm[:, g, :], 4), ALU.min, ALU.mult,
                                 reads=[eex_r[g], cbm_r], writes=[Mm_r[g]])
                            act(Eh[g][:], pbc3, AF.Exp, reads=[pbc_r], writes=[Eh_r[g]])
                            vtt(Cs[g][:], Eh[g][:], bc_mid(CTt[:, g, bsl], 4), ALU.mult,
                                reads=[Eh_r[g], CT_r[g]], writes=[Cs_r[g]])
                        xs3 = xstok[:, blk, :].rearrange("p (h d) -> p h d", h=8)
                        vtt(Xs[:], xs3, bc_last(dtt[:, blk, :], 64), ALU.mult,
                            reads=[xstok_r[blk], dtt_r], writes=[Xs_r])
                        vtt(Xd[:], xs3, bc_last(xsc[:, blk, :], 64), ALU.mult,
                            reads=[xstok_r[blk], xsc_r], writes=[Xd_r])
                        psY, psY_r = psum()
                        for h in range(8):
                            g, k = h // 4, h % 4
                            o = psY[(h % 2) * 64:(h % 2 + 1) * 64, (h // 2) * 128:(h // 2 + 1) * 128]
                            mm(o, Xs[:, h, :], Mm[g][:, k, :], True, False, reads=[Xs_r, Mm_r[g]], writes=[psY_r])
                            mm(o, xstok[:, blk, h * 64:(h + 1) * 64], DI[:, h, :], False, False,
                               reads=[xstok_r[blk], DI_r], writes=[psY_r])
                            mm(o, prevbf[:, h, :], Cs[g][:, k, :], False, True, reads=[prevbf_r, Cs_r[g]], writes=[psY_r])
                        vtt(yg[:, :, bsl], psY[:, :].rearrange("p (c l) -> p c l", c=4), zs[:, :, bsl], ALU.mult,
                            reads=[psY_r] + zs_r, writes=yg_r)
                        psS, psS_r = psum()
                        for g in range(2):
                            mm(psS[:, g * 256:(g + 1) * 256], Btok[:, blk, g * 128:(g + 1) * 128],
                               Xd[:, 4 * g:4 * g + 4, :].rearrange("p a b -> p (a b)"), True, True,
                               reads=[Btok_r[blk], Xd_r], writes=[psS_r])
                        vtt(ptmp[:], prev32[:], bc_last(cd[:, blk, :], 64), ALU.mult,
                            reads=[prev32_r, cd_r], writes=[ptmp_r])
                        vtt(prev32[:], ptmp[:], psS[:, :].rearrange("p (h d) -> p h d", h=8), ALU.add,
                            reads=[ptmp_r, psS_r], writes=[prev32_r])
                        act(prevbf[:], prev32[:], AF.Copy, reads=[prev32_r], writes=[prevbf_r])
                    rmsnorm_tile([yg[:, c, :] for c in range(4)], yg_r, 4,
                                 [pr[:, P_SNW + c:P_SNW + c + 1] for c in range(4)], pr_r,
                                 lambda c: zs[:, c, :], zs_r, 512, (sqb, sqb_r, lnv, lnv_r, rstd, rstd_r))
                K.barrier()
                with ExitStack() as ph:
                    tb = lambda name, shape, dt: ph.enter_context(nc.sbuf_tensor(uname(name), shape, dt))
                    ez = [tb("ez%d" % i, [128, TT], F32) for i in range(2)]; ez_r = R(2)
                    spb = [tb("spb%d" % i, [128, TT], BF16) for i in range(2)]; spb_r = R(2)
                    wTt = [tb("wTt%d" % i, [128, TT], BF16) for i in range(2)]; wT_r = R(2)
                    S32 = tb("S32", [128, TT], F32); S32_r = R()
                    Sbf = tb("Sbf", [128, TT], BF16); Sbf_r = R()
                    it = 0
                    for c in range(2):
                        psO, psO_r = psum(hold=True)
                        for hh in range(2):
                            hd = 2 * c + hh
                            po = slice(hh * 64, (hh + 1) * 64)
                            nkb = NBT * (tt + 1)
                            for i, kb in enumerate(range(nkb - 1, -1, -1)):
                                ksl = slice(kb * 128, (kb + 1) * 128)
                                ktt = kb // NBT
                                diag = kb >= NBT * tt
                                mi = kb - NBT * tt
                                b2 = it % 2
                                it += 1
                                psZ, psZ_r = psum()
                                mm(psZ[:, :], kT[po, c, ksl], qT[po, c, :], True, True,
                                   reads=[kT_r[c][ktt], qT_r[c]], writes=[psZ_r])
                                act(ez[b2][:, :], psZ[:, :], AF.Exp, reads=[psZ_r], writes=[ez_r[b2]])
                                act(spb[b2][:, :], ez[b2][:, :], AF.Ln, reads=[ez_r[b2]], writes=[spb_r[b2]], bias=1.0)
                                if diag:
                                    vtt(spb[b2][:, :], spb[b2][:, :], sbmask[mi], ALU.mult,
                                        reads=[spb_r[b2], cb_r], writes=[spb_r[b2]])
                                psL, psL_r = psum()
                                mm(psL[:, :], kT[po, c, ksl], qT[po, c, :], True, False,
                                   reads=[kT_r[c][ktt], qT_r[c]], writes=[psL_r])
                                mm(psL[:, :], negtri_b, spb[b2][:, :], False, i == 0,
                                   reads=[cb_r, spb_r[b2]], writes=[psL_r])
                                if i > 0:
                                    mm(psL[:, :], negones_b, Sbf[:, :], False, True, reads=[cb_r, Sbf_r], writes=[psL_r])
                                act(wTt[b2][:, :], psL[:, :], AF.Exp, reads=[psL_r], writes=[wT_r[b2]])
                                if diag:
                                    vtt(wTt[b2][:, :], wTt[b2][:, :], sbmask[mi], ALU.mult,
                                        reads=[wT_r[b2], cb_r], writes=[wT_r[b2]])
                                mm(psO[po, :], vtok[:, kb, hd * 64:(hd + 1) * 64], wTt[b2][:, :], i == 0, kb == 0,
                                   reads=[vtok_r[kb], wT_r[b2]], writes=[psO_r])
                                if kb > 0:
                                    if i == 0:
                                        vcopy(S32[:, :], spb[b2][:, :], reads=[spb_r[b2]], writes=[S32_r])
                                    else:
                                        vtt(S32[:, :], S32[:, :], spb[b2][:, :], ALU.add,
                                            reads=[S32_r, spb_r[b2]], writes=[S32_r])
                                    vcopy(Sbf[:, :], S32[:, :], reads=[S32_r], writes=[Sbf_r])
                        act(ysb[:, c, :], psO[:, :], AF.Copy, reads=[psO_r], writes=[ysb_r[c]])
                        psum_release(psO)
                K.barrier()
                ycat = [(zs[:, c, :], zs_r[c]) for c in range(4)] + [(ysb[:, c, :], ysb_r[c]) for c in range(2)] + \
                       [(ypl[:, c, :], ypl_r[c]) for c in range(2)]
                if debug and l == 0:
                    dby_v = dby_d.rearrange("(c p) t -> p c t", p=128)
                    for kc in range(8):
                        K.dma("pool", dby_v[:, kc, tsl], ycat[kc][0], reads=[ycat[kc][1]], writes=[dbg_res])
                for oc in range(8):
                    wa, wa_r = SA.get()
                    ps, ps_r = psum()
                    for kc in range(8):
                        mm(ps[:, :], wa[:, kc, :], ycat[kc][0], kc == 0, kc == 7,
                           reads=[wa_r, ycat[kc][1]], writes=[ps_r])
                    vtt(xT[:, oc, tsl], xT[:, oc, tsl], ps[:, :], ALU.add,
                        reads=[xT_r[oc][tt], ps_r], writes=[xT_r[oc][tt]])
                if debug and l == 0:
                    dbx_v = dbx_d.rearrange("(c p) t -> p c t", p=128)
                    for oc in range(8):
                        K.dma("sp", dbx_v[:, oc, tsl], xT[:, oc, tsl], reads=[xT_r[oc][tt]], writes=[dbg_res])
                with ExitStack() as ph:
                    tb = lambda name, shape, dt: ph.enter_context(nc.sbuf_tensor(uname(name), shape, dt))
                    sqb = tb("sqb3", [128, 8, TT], BF16); sqb_r = R(8)
                    lnv = tb("lnv3", [128, TT], F32); lnv_r = R()
                    rstd = tb("rstd3", [128, TT], F32); rstd_r = R()
                    aT = tb("aT", [128, NFC, TT], BF16); aT_r = R(NFC)
                    sg = [tb("sg%d" % i, [128, TT], F32) for i in range(2)]; sg_r = R(2)
                    rmsnorm_tile([xT[:, c, tsl] for c in range(8)], [xT_r[c][tt] for c in range(8)], 8,
                                 [pr[:, P_N2W + c:P_N2W + c + 1] for c in range(8)], pr_r,
                                 lambda c: hT[:, c, :], hT_r, D, (sqb, sqb_r, lnv, lnv_r, rstd, rstd_r))
                    for fc in range(NFC):
                        wg, wg_r = SA.get()
                        wu, wu_r = SA.get()
                        psG, psG_r = psum()
                        psU, psU_r = psum()
                        for kc in range(8):
                            mm(psG[:, :], wg[:, kc, :], hT[:, kc, :], kc == 0, kc == 7, reads=[wg_r, hT_r[kc]], writes=[psG_r])
                        for kc in range(8):
                            mm(psU[:, :], wu[:, kc, :], hT[:, kc, :], kc == 0, kc == 7, reads=[wu_r, hT_r[kc]], writes=[psU_r])
                        b2 = fc % 2
                        act(sg[b2][:, :], psG[:, :], AF.Silu, reads=[psG_r], writes=[sg_r[b2]])
                        vtt(aT[:, fc, :], sg[b2][:, :], psU[:, :], ALU.mult, reads=[sg_r[b2], psU_r], writes=[aT_r[fc]])
                    for oc in range(8):
                        wd, wd_r = SD.get()
                        ps, ps_r = psum()
                        for fc in range(NFC):
                            mm(ps[:, :], wd[:, fc, :], aT[:, fc, :], fc == 0, fc == NFC - 1,
                               reads=[wd_r, aT_r[fc]], writes=[ps_r])
                        vtt(xT[:, oc, tsl], xT[:, oc, tsl], ps[:, :], ALU.add,
                            reads=[xT_r[oc][tt], ps_r], writes=[xT_r[oc][tt]])
                    if l == depth - 1:
                        ost = [tb("ost%d" % i, [128, 4, TT], F32) for i in range(2)]; ost_r = [R(4), R(4)]
                        out_v = out_d.rearrange("(c p) t -> p c t", p=128)
                        rmsnorm_tile([xT[:, c, tsl] for c in range(8)], [xT_r[c][tt] for c in range(8)], 8,
                                     [fnw[:, c:c + 1] for c in range(8)], fnw_r,
                                     lambda c: ost[c // 4][:, c % 4, :], ost_r[0] + ost_r[1], D,
                                     (sqb, sqb_r, lnv, lnv_r, rstd, rstd_r))
                        for c in range(8):
                            tok = K.dma("sp", out_v[:, c, tsl], ost[c // 4][:, c % 4, :],
                                        reads=[ost_r[c // 4][c % 4]], writes=[out_res[c * NTT + tt]])
                K.barrier()
        K.final_wait("sp", out_res + [dbg_res])
    return nc


def _tile_k(w):
    K_, N_ = w.shape
    return np.ascontiguousarray(w.reshape(K_ // 128, 128, N_).transpose(1, 0, 2)).reshape(128, -1)


def _consts():
    j = np.arange(128)[:, None]
    l = np.arange(128)[None, :]
    cf = np.zeros((128, CF_N), np.float32)
    cf[:, CF_IDENT:CF_IDENT + 128] = (j == l)
    cf[:, CF_ONES:CF_ONES + 128] = 1.0
    cf[:, CF_TRI:CF_TRI + 128] = (j <= l)
    cf[:, CF_RCFIX:CF_RCFIX + 16] = 1.0 / (np.arange(16)[None, :] + 1.0)
    cb = np.zeros((128, CB_N), np.float32)
    cb[:, CB_IDENT:CB_IDENT + 128] = (j == l)
    cb[:, CB_ONES:CB_ONES + 128] = 1.0
    cb[:, CB_NEGTRI:CB_NEGTRI + 128] = -1.0 * (j >= l)
    cb[:, CB_NEGONES:CB_NEGONES + 128] = -1.0
    t = np.arange(512)[None, :]
    for i in range(4):
        cb[:, CB_MASK + i * 512:CB_MASK + (i + 1) * 512] = ((128 * i + j) < t)
    return cf, cb


def prep_weights(inp, depth):
    f = lambda a: np.asarray(a, dtype=np.float32)
    w_in, w_out = f(inp["w_in"]), f(inp["w_out"])
    w_gate, w_up, w_down = f(inp["w_gate"]), f(inp["w_up"]), f(inp["w_down"])
    fm_cols = [c * 128 for c in range(4)] + [512 + c * 128 for c in range(8)] + [1544, 1672, 1800, 1928, 2312, 2440]
    w_fm = np.empty((depth, 18, 128, 1024), np.float32)
    w_tm = np.empty((depth, 128, 8 * 264), np.float32)
    w_o = np.empty((depth, 8, 128, 1024), np.float32)
    w_gu = np.empty((depth, NFC, 2, 128, 1024), np.float32)
    w_dn = np.empty((depth, 8, 128, NFC * 128), np.float32)
    par = np.zeros((depth, 128, NPAR), np.float32)
    pw = np.zeros((depth, 128, 256), np.float32)
    for l in range(depth):
        for i, c0 in enumerate(fm_cols):
            w_fm[l, i] = _tile_k(w_in[l][:, c0:c0 + 128])
        w_tm[l] = _tile_k(np.concatenate([w_in[l][:, 2056:2312], w_in[l][:, 1536:1544]], axis=1))
        for oc in range(8):
            w_o[l, oc] = _tile_k(w_out[l][:, oc * 128:(oc + 1) * 128])
            w_dn[l, oc] = _tile_k(w_down[l][:, oc * 128:(oc + 1) * 128])
        for fc in range(NFC):
            w_gu[l, fc, 0] = _tile_k(w_gate[l][:, fc * 128:(fc + 1) * 128])
            w_gu[l, fc, 1] = _tile_k(w_up[l][:, fc * 128:(fc + 1) * 128])
        cm = lambda v: np.asarray(v, np.float32).reshape(-1, 128).T
        par[l, :, P_N1W:P_N1W + 8] = cm(inp["norm1_w"][l])
        par[l, :, P_N2W:P_N2W + 8] = cm(inp["norm2_w"][l])
        cw = f(inp["conv_w"][l])
        for i in range(4):
            par[l, :, P_CW + i:P_CW + 32:4] = cm(cw[i])
        par[l, :, P_CB:P_CB + 8] = cm(inp["conv_b"][l])
        par[l, :, P_SNW:P_SNW + 4] = cm(inp["ssd_norm_w"][l])
        par[l, :, P_PB:P_PB + 2] = cm(f(inp["pool_b"][l]).reshape(-1))
        par[l, :, P_PS:P_PS + 2] = cm(inp["pool_scale"][l])
        par[l, :, P_DTB:P_DTB + 8] = f(inp["dt_bias"][l])[None, :]
        par[l, :, P_ALOG:P_ALOG + 8] = f(inp["a_log"][l])[None, :]
        par[l, :, P_DSK:P_DSK + 8] = f(inp["d_skip"][l])[None, :]
        pwl = f(inp["pool_w"][l])
        for i in range(2):
            for hh in range(2):
                pw[l, hh * 64:(hh + 1) * 64, i * 128 + hh * 64:i * 128 + (hh + 1) * 64] = pwl[2 * i + hh]
    cf, cb = _consts()
    fnw = np.ascontiguousarray(f(inp["final_norm_w"]).reshape(8, 128).T)
    return dict(w_fm=w_fm, w_tm=w_tm, w_o=w_o, w_gu=w_gu, w_dn=w_dn, par=par, pw=pw, fnw=fnw, cf=cf, cb=cb)


_PROG = {}


def run(inputs, depth=DEPTH, cores=NCORES, debug=False):
    if (depth, debug) not in _PROG:
        _PROG[(depth, debug)] = build_program(depth, debug)
    nc = _PROG[(depth, debug)]
    wts = prep_weights(inputs, depth)
    x = np.asarray(inputs["x"], dtype=np.float32)
    in_maps = []
    for b in range(cores):
        m = dict(wts)
        m["xT"] = np.ascontiguousarray(x[b].T)
        in_maps.append(m)
    res = run_bass_kernel_spmd(nc, in_maps, core_ids=list(range(cores)))
    out = np.stack([np.ascontiguousarray(r["outT"].T) for r in res.results], axis=0)
    if debug:
        return out.astype(np.float32), [dict(r) for r in res.results]
    return out.astype(np.float32)


def kernel(**inputs):
    return run(inputs, DEPTH, NCORES)
```

```python
import numpy as np
from contextlib import ExitStack
import concourse.bass as bass
import concourse.mybir as mybir
from concourse.bass_utils import run_bass_kernel_spmd

F32 = mybir.dt.float32
BF16 = mybir.dt.bfloat16
AF = mybir.ActivationFunctionType
ALU = mybir.AluOpType

D = 1024
T = 2048
DEPTH = 4
NCORES = 8
DFF = 2816
NFC = DFF // 128
TT = 512
NTT = T // TT
NBT = TT // 128
NB = T // 128
EPS = 1e-6
NPAR = 88

P_N1W, P_N2W, P_CW, P_CB, P_SNW, P_PB, P_PS, P_DTB, P_ALOG, P_DSK = 0, 8, 16, 48, 56, 60, 62, 64, 72, 80
CF_IDENT, CF_ONES, CF_TRI, CF_RCFIX, CF_N = 0, 128, 256, 384, 400
CB_IDENT, CB_ONES, CB_NEGTRI, CB_NEGONES, CB_MASK, CB_N = 0, 128, 256, 384, 512, 512 + 4 * 512


class Res:
    __slots__ = ("w", "r")

    def __init__(self):
        self.w = None
        self.r = {}


class Eng:
    def __init__(self, name, h, sem):
        self.name, self.h, self.sem = name, h, sem
        self.count = 0
        self.waited = {}


class KB:
    def __init__(self, nc, stack, n_dma_sems=40):
        self.nc = nc
        self.eng = {}
        for name, h in (("pe", nc.tensor), ("act", nc.scalar), ("dve", nc.vector), ("pool", nc.gpsimd), ("sp", nc.sync)):
            sem = stack.enter_context(nc.semaphore("sem_" + name))
            self.eng[name] = Eng(name, h, sem)
        self.dsem = [stack.enter_context(nc.semaphore("dsem%d" % i)) for i in range(n_dma_sems)]
        self.dval = [0] * n_dma_sems
        self.dnext = 0
        self.semkey = {}

    def _key(self, sem):
        return id(sem)

    def _deps(self, e, reads, writes):
        need = {}

        def add(tok, raw):
            sem, val = tok
            if sem is e.sem and (e.name == "pe" or not raw):
                return
            k = id(sem)
            if k not in need or need[k][1] < val:
                need[k] = (sem, val)

        for r in reads:
            if r.w is not None:
                add(r.w, True)
        for w in writes:
            if w.w is not None:
                add(w.w, False)
            for k, tok in w.r.items():
                add(tok, False)
        for k, (sem, val) in need.items():
            if e.waited.get(k, 0) < val:
                e.h.wait_ge(sem, val)
                e.waited[k] = val

    def _mark(self, tok, reads, writes):
        k = id(tok[0])
        for r in reads:
            r.r[k] = tok
        for w in writes:
            w.w = tok
            w.r = {}

    def op(self, en, fn, reads=(), writes=()):
        e = self.eng[en]
        self._deps(e, reads, writes)
        ins = fn(e.h)
        e.count += 1
        ins.then_inc(e.sem, 1)
        tok = (e.sem, e.count)
        self._mark(tok, reads, writes)
        return tok

    def dma(self, qn, out, in_, reads=(), writes=()):
        q = self.eng[qn]
        slot = self.dnext
        self.dnext = (slot + 1) % len(self.dsem)
        sem = self.dsem[slot]
        self._deps(q, reads, writes)
        if self.dval[slot] > 0 and q.waited.get(id(sem), 0) < self.dval[slot]:
            q.h.wait_ge(sem, self.dval[slot])
            q.waited[id(sem)] = self.dval[slot]
        ins = q.h.dma_start(out=out, in_=in_)
        self.dval[slot] += 16
        ins.then_inc(sem, 16)
        tok = (sem, self.dval[slot])
        self._mark(tok, reads, writes)
        return tok

    def barrier(self):
        toks = [(e.sem, e.count) for e in self.eng.values() if e.count > 0]
        for e in self.eng.values():
            for sem, val in toks:
                if sem is e.sem:
                    continue
                if e.waited.get(id(sem), 0) < val:
                    e.h.wait_ge(sem, val)
                    e.waited[id(sem)] = val

    def final_wait(self, en, reslist):
        e = self.eng[en]
        self._deps(e, reslist, [])


def build_program(depth=DEPTH, debug=False):
    nc = bass.Bass("TRN2", target_bir_lowering=False)
    dt_ = lambda n, s: nc.dram_tensor(n, s, F32, kind="ExternalInput").ap()
    xT_d = dt_("xT", [D, T])
    wfm_d = dt_("w_fm", [depth, 18, 128, 8 * 128])
    wtm_d = dt_("w_tm", [depth, 128, 8 * 264])
    wout_d = dt_("w_o", [depth, 8, 128, 8 * 128])
    wgu_d = dt_("w_gu", [depth, NFC, 2, 128, 8 * 128])
    wdn_d = dt_("w_dn", [depth, 8, 128, NFC * 128])
    par_d = dt_("par", [depth, 128, NPAR])
    pw_d = dt_("pw", [depth, 128, 2 * 128])
    fnw_d = dt_("fnw", [128, 8])
    cf_d = dt_("cf", [128, CF_N])
    cb_d = dt_("cb", [128, CB_N])
    out_d = nc.dram_tensor("outT", [D, T], F32, kind="ExternalOutput").ap()
    if debug:
        dby_d = nc.dram_tensor("dbg_y", [D, T], F32, kind="ExternalOutput").ap()
        dbx_d = nc.dram_tensor("dbg_x", [D, T], F32, kind="ExternalOutput").ap()

    with ExitStack() as st:
        K = KB(nc, st)
        uid = [0]

        def uname(name):
            uid[0] += 1
            return "s_%s_%d" % (name, uid[0])

        sb = lambda name, shape, dt: st.enter_context(nc.sbuf_tensor(uname(name), shape, dt))

        def R(n=None):
            return Res() if n is None else [Res() for _ in range(n)]

        xT = sb("xT", [128, 8, T], F32)
        xT_r = [[Res() for _ in range(NTT)] for _ in range(8)]
        kT = sb("kT", [128, 2, T], BF16)
        kT_r = [[Res() for _ in range(NTT)] for _ in range(2)]
        vtok = sb("vtok", [128, NB, 256], BF16)
        vtok_r = R(NB)
        hT = sb("hT", [128, 8, TT], BF16)
        hT_r = R(8)
        zs = sb("zs", [128, 4, TT], BF16)
        zs_r = R(4)
        ysb = sb("ysb", [128, 2, TT], BF16)
        ysb_r = R(2)
        ypl = sb("ypl", [128, 2, TT], BF16)
        ypl_r = R(2)
        xstok = sb("xstok", [128, NBT, 512], BF16)
        xstok_r = R(NBT)
        BTt = sb("BT", [128, 2, TT], BF16)
        BT_r = R(2)
        Btok = sb("Btok", [128, NBT, 256], BF16)
        Btok_r = R(NBT)
        CTt = sb("CT", [128, 2, TT], BF16)
        CT_r = R(2)
        qT = sb("qT", [128, 2, TT], BF16)
        qT_r = R(2)
        cf = sb("cf", [128, CF_N], F32)
        cf_r = R()
        cb = sb("cb", [128, CB_N], BF16)
        cb_r = R()
        par = [sb("par%d" % i, [128, NPAR], F32) for i in range(2)]
        par_r = R(2)
        fnw = sb("fnw", [128, 8], F32)
        fnw_r = R()
        DI = sb("DI", [128, 8, 128], BF16)
        DI_r = R()
        pwbd = sb("pwbd", [128, 2, 128], BF16)
        pwbd_r = R()
        arep = sb("arep", [128, 8], F32)
        arep_r = R()
        prev32 = sb("prev32", [128, 8, 64], F32)
        prev32_r = R()
        prevbf = sb("prevbf", [128, 8, 64], BF16)
        prevbf_r = R()
        chalo = sb("chalo", [128, 8, 4], F32)
        chalo_r = R(8)
        pst = sb("pst", [128, 2, 16 + TT], F32)
        pst_r = R(2)
        smalls = {}
        for nm in ("dtr", "ee", "dtt", "dA", "acum", "tot", "cd", "dd", "dte", "xsc"):
            smalls[nm] = (sb("sm_" + nm, [128, NBT, 8], F32), Res())
        NA, NTM, ND = 6, 1, 2
        wA = [sb("wA%d" % i, [128, 8, 128], BF16) for i in range(NA)]
        wA_r = R(NA)
        wTm = [sb("wT%d" % i, [128, 8, 264], BF16) for i in range(NTM)]
        wTm_r = R(NTM)
        wD = [sb("wD%d" % i, [128, NFC, 128], BF16) for i in range(ND)]
        wD_r = R(ND)
        psb = [st.enter_context(nc.psum_tensor("ps%d" % i, [128, 512], F32)) for i in range(8)]
        psb_r = R(8)
        psn = [0]

        held = set()

        def psum(hold=False):
            i = psn[0]
            while i in held:
                i = (i + 1) % 8
            psn[0] = (i + 1) % 8
            if hold:
                held.add(i)
            return psb[i], psb_r[i]

        def psum_release(ps):
            for i in range(8):
                if psb[i] is ps:
                    held.discard(i)

        ident_f = cf[:, CF_IDENT:CF_IDENT + 128]
        ones_f = cf[:, CF_ONES:CF_ONES + 128]
        tri_f = cf[:, CF_TRI:CF_TRI + 128]
        rcfix = cf[:, CF_RCFIX:CF_RCFIX + 16]
        ident_b = cb[:, CB_IDENT:CB_IDENT + 128]
        ones_b = cb[:, CB_ONES:CB_ONES + 128]
        negtri_b = cb[:, CB_NEGTRI:CB_NEGTRI + 128]
        negones_b = cb[:, CB_NEGONES:CB_NEGONES + 128]
        sbmask = [cb[:, CB_MASK + i * 512:CB_MASK + (i + 1) * 512] for i in range(4)]

        K.dma("sp", cf[:], cf_d[:], writes=[cf_r])
        K.dma("pool", cb[:], cb_d[:], writes=[cb_r])
        K.dma("sp", fnw[:], fnw_d[:], writes=[fnw_r])
        xT_v = xT_d.rearrange("(c p) t -> p c t", p=128)
        for c in range(8):
            K.dma("sp", xT[:, c, :], xT_v[:, c, :], writes=xT_r[c])

        loadsA, loadsT, loadsD = [], [], []
        for l in range(depth):
            for tt in range(NTT):
                loadsT.append(wtm_d[l])
                for oc in range(18):
                    loadsA.append(wfm_d[l, oc])
                for oc in range(8):
                    loadsA.append(wout_d[l, oc])
                for fc in range(NFC):
                    loadsA.append(wgu_d[l, fc, 0])
                    loadsA.append(wgu_d[l, fc, 1])
                for oc in range(8):
                    loadsD.append(wdn_d[l, oc])

        class Stream:
            def __init__(self, loads, tiles, res, shp):
                self.loads, self.tiles, self.res, self.shp = loads, tiles, res, shp
                self.issued = 0
                self.used = 0

            def get(self):
                i = self.used
                n = len(self.tiles)
                while self.issued < len(self.loads) and self.issued <= i + n - self.shp:
                    j = self.issued
                    tl = self.tiles[j % n]
                    K.dma("pool", tl[:].rearrange("p a b -> p (a b)"), self.loads[j], writes=[self.res[j % n]])
                    self.issued += 1
                self.used += 1
                return self.tiles[i % n], self.res[i % n]

        SA = Stream(loadsA, wA, wA_r, 2)
        STm = Stream(loadsT, wTm, wTm_r, 1)
        SD = Stream(loadsD, wD, wD_r, 1)

        def mm(out, lhsT, rhs, start, stop, reads, writes):
            return K.op("pe", lambda h: h.matmul(out, lhsT, rhs, start=start, stop=stop), reads=reads, writes=writes)

        def act(out, in_, func, reads, writes, bias=None, scale=None):
            kw = {}
            if bias is not None:
                kw["bias"] = bias
            if scale is not None:
                kw["scale"] = scale
            return K.op("act", lambda h: h.activation(out=out, in_=in_, func=func, **kw), reads=reads, writes=writes)

        def vtt(out, in0, in1, op, reads, writes, en="dve"):
            return K.op(en, lambda h: h.tensor_tensor(out, in0, in1, op), reads=reads, writes=writes)

        def vts(out, in0, s1, s2, op0, op1, reads, writes, en="dve"):
            if s2 is None:
                return K.op(en, lambda h: h.tensor_scalar(out, in0, s1, None, op0), reads=reads, writes=writes)
            return K.op(en, lambda h: h.tensor_scalar(out, in0, s1, s2, op0, op1), reads=reads, writes=writes)

        def vstt(out, in0, sc, in1, op0, op1, reads, writes, en="dve"):
            return K.op(en, lambda h: h.scalar_tensor_tensor(out, in0, sc, in1, op0, op1), reads=reads, writes=writes)

        def vcopy(out, in_, reads, writes, en="dve"):
            return K.op(en, lambda h: h.tensor_copy(out, in_), reads=reads, writes=writes)

        def vmemset(ap, val, writes, en="dve"):
            return K.op(en, lambda h: h.memset(ap, val), writes=writes)

        def bc_mid(ap2, n):
            return ap2.unsqueeze(1).broadcast_to([ap2.shape[0], n, ap2.shape[1]])

        def bc_last(ap2, n):
            return ap2.unsqueeze(2).broadcast_to([ap2.shape[0], ap2.shape[1], n])

        dbg_outs = {}
        marks = []

        def mark(name):
            marks.append((name, {n: e.count for n, e in K.eng.items()}))
        out_res = [Res() for _ in range(8 * NTT)]
        dbg_res = Res()

        def rmsnorm_tile(src_chunks, src_res, nchunk, wcols, wres, dst_fn, dst_res, nfeat, tmp):
            sqb, sqb_r, lnv, lnv_r, rstd, rstd_r = tmp
            for c in range(nchunk):
                act(sqb[:, c, :], src_chunks[c], AF.Square, reads=[src_res[c]], writes=[sqb_r[c]])
            ps, ps_r = psum()
            for c in range(nchunk):
                mm(ps[:, :], ones_b, sqb[:, c, :], c == 0, c == nchunk - 1, reads=[cb_r, sqb_r[c]], writes=[ps_r])
            act(lnv[:, :], ps[:, :], AF.Ln, reads=[ps_r], writes=[lnv_r], bias=EPS, scale=1.0 / nfeat)
            act(rstd[:, :], lnv[:, :], AF.Exp, reads=[lnv_r], writes=[rstd_r], scale=-0.5)
            for c in range(nchunk):
                vstt(dst_fn(c), src_chunks[c], wcols[c], rstd[:, :], ALU.mult, ALU.mult,
                     reads=[src_res[c], wres, rstd_r], writes=[dst_res[c]])

        for l in range(depth):
            pr = par[l % 2]
            pr_r = par_r[l % 2]
            K.dma("sp", pr[:], par_d[l], writes=[pr_r])
            K.dma("pool", pwbd[:].rearrange("p a b -> p (a b)"), pw_d[l], writes=[pwbd_r])
            vmemset(prev32[:], 0.0, [prev32_r])
            vmemset(prevbf[:], 0.0, [prevbf_r])
            vmemset(chalo[:], 0.0, chalo_r)
            vmemset(pst[:], 0.0, pst_r)
            act(arep[:, :], pr[:, P_ALOG:P_ALOG + 8], AF.Exp, reads=[pr_r], writes=[arep_r])
            vts(arep[:, :], arep[:, :], -1.0, None, ALU.mult, None, reads=[arep_r], writes=[arep_r])
            for h in range(8):
                vts(DI[:, h, :], ident_f, pr[:, P_DSK + h:P_DSK + h + 1], None, ALU.mult, None,
                    reads=[cf_r, pr_r], writes=[DI_r])

            for tt in range(NTT):
                tsl = slice(tt * TT, (tt + 1) * TT)
                mark('P1 l%d t%d' % (l, tt))
                with ExitStack() as ph:
                    tb = lambda name, shape, dt: ph.enter_context(nc.sbuf_tensor(uname(name), shape, dt))
                    sqb = tb("sqb", [128, 8, TT], BF16); sqb_r = R(8)
                    lnv = tb("lnv", [128, TT], F32); lnv_r = R()
                    rstd = tb("rstd", [128, TT], F32); rstd_r = R()
                    cst = [tb("cst%d" % i, [128, 4 + TT], F32) for i in range(2)]; cst_r = R(2)
                    ctmp = [tb("ctmp%d" % i, [128, TT], F32) for i in range(2)]; ctmp_r = R(2)
                    cact = [tb("cact%d" % i, [128, TT], BF16) for i in range(2)]; cact_r = R(2)
                    s2 = tb("s2", [128, 16 + TT], F32); s2_r = R()
                    s4 = tb("s4", [128, 16 + TT], F32); s4_r = R()
                    s8 = tb("s8", [128, 16 + TT], F32); s8_r = R()
                    s16 = tb("s16", [128, 16 + TT], F32); s16_r = R()
                    pfix = tb("pfix", [128, 16], F32); pfix_r = R()
                    pooled = [tb("pooled%d" % i, [128, TT], BF16) for i in range(2)]; pooled_r = R(2)
                    dtr, dtr_r = smalls["dtr"]

                    rmsnorm_tile([xT[:, c, tsl] for c in range(8)], [xT_r[c][tt] for c in range(8)], 8,
                                 [pr[:, P_N1W + c:P_N1W + c + 1] for c in range(8)], pr_r,
                                 lambda c: hT[:, c, :], hT_r, D, (sqb, sqb_r, lnv, lnv_r, rstd, rstd_r))

                    wt, wt_r = STm.get()
                    for blk in range(NBT):
                        ps, ps_r = psum()
                        bsl = slice(blk * 128, (blk + 1) * 128)
                        for kc in range(8):
                            mm(ps[:, 0:264], hT[:, kc, bsl], wt[:, kc, :], kc == 0, kc == 7,
                               reads=[hT_r[kc], wt_r], writes=[ps_r])
                        gb = tt * NBT + blk
                        act(vtok[:, gb, :], ps[:, 0:256], AF.Copy, reads=[ps_r], writes=[vtok_r[gb]])
                        vtt(dtr[:, blk, :], ps[:, 256:264], pr[:, P_DTB:P_DTB + 8], ALU.add,
                            reads=[ps_r, pr_r], writes=[dtr_r])

                    for oc in range(18):
                        wa, wa_r = SA.get()
                        ps, ps_r = psum()
                        for kc in range(8):
                            mm(ps[:, :], wa[:, kc, :], hT[:, kc, :], kc == 0, kc == 7,
                               reads=[wa_r, hT_r[kc]], writes=[ps_r])
                        if oc < 4:
                            act(zs[:, oc, :], ps[:, :], AF.Silu, reads=[ps_r], writes=[zs_r[oc]])
                        elif oc < 12:
                            j = oc - 4
                            sl_ = j % 2
                            cs, cs_r = cst[sl_], cst_r[sl_]
                            ct, ct_r = ctmp[sl_], ctmp_r[sl_]
                            ca, ca_r = cact[sl_], cact_r[sl_]
                            vcopy(cs[:, 0:3], chalo[:, j, 0:3], reads=[chalo_r[j]], writes=[cs_r])
                            act(cs[:, 3:3 + TT], ps[:, :], AF.Copy, reads=[ps_r], writes=[cs_r])
                            vcopy(chalo[:, j, 0:3], cs[:, TT:TT + 3], reads=[cs_r], writes=[chalo_r[j]])
                            cw = lambda i: pr[:, P_CW + j * 4 + i:P_CW + j * 4 + i + 1]
                            vts(ct[:, :], cs[:, 0:TT], cw(0), pr[:, P_CB + j:P_CB + j + 1], ALU.mult, ALU.add,
                                reads=[cs_r, pr_r], writes=[ct_r])
                            for i in range(1, 4):
                                vstt(ct[:, :], cs[:, i:i + TT], cw(i), ct[:, :], ALU.mult, ALU.add,
                                     reads=[cs_r, pr_r, ct_r], writes=[ct_r])
                            if j < 4:
                                act(ca[:, :], ct[:, :], AF.Silu, reads=[ct_r], writes=[ca_r])
                                src, src_r = ca, ca_r
                            elif j < 6:
                                g = j - 4
                                act(BTt[:, g, :], ct[:, :], AF.Silu, reads=[ct_r], writes=[BT_r[g]])
                                src, src_r = None, BT_r[g]
                            else:
                                g = j - 6
                                act(CTt[:, g, :], ct[:, :], AF.Silu, reads=[ct_r], writes=[CT_r[g]])
                            if j < 6:
                                pt, pt_r = psum()
                                ptb = pt[:, :].bitcast(BF16)
                                for blk in range(NBT):
                                    bsl = slice(blk * 128, (blk + 1) * 128)
                                    srcap = ca[:, bsl] if j < 4 else BTt[:, j - 4, bsl]
                                    K.op("pe", lambda h, o=ptb[:, blk * 128:(blk + 1) * 128], s=srcap: h.transpose(o, s, ident_b),
                                         reads=[src_r, cb_r], writes=[pt_r])
                                if j < 4:
                                    vcopy(xstok[:, :, j * 128:(j + 1) * 128],
                                          ptb[:, 0:NBT * 128].rearrange("p (b c) -> p b c", b=NBT),
                                          reads=[pt_r], writes=xstok_r)
                                else:
                                    g = j - 4
                                    vcopy(Btok[:, :, g * 128:(g + 1) * 128],
                                          ptb[:, 0:NBT * 128].rearrange("p (b c) -> p b c", b=NBT),
                                          reads=[pt_r], writes=Btok_r)
                        elif oc < 14:
                            i = oc - 12
                            K.op('act', lambda h, o=qT[:, i, :], s=ps[:, :]: h.mul(o, s, 0.125), reads=[ps_r], writes=[qT_r[i]])
                        elif oc < 16:
                            i = oc - 14
                            act(kT[:, i, tsl], ps[:, :], AF.Copy, reads=[ps_r], writes=[kT_r[i][tt]])
                        else:
                            i = oc - 16
                            A = pst[:, i, :]
                            A_r = pst_r[i]
                            W = 16 + TT
                            act(A[:, 16:W], ps[:, :], AF.Copy, reads=[ps_r], writes=[A_r])
                            vtt(s2[:, 1:W], A[:, 1:W], A[:, 0:W - 1], ALU.add, reads=[A_r], writes=[s2_r])
                            vtt(s4[:, 3:W], s2[:, 3:W], s2[:, 1:W - 2], ALU.add, reads=[s2_r], writes=[s4_r])
                            if i == 0:
                                lev = [(s2, s2_r, 2), (s4, s4_r, 4)]
                            else:
                                vtt(s8[:, 7:W], s4[:, 7:W], s4[:, 3:W - 4], ALU.add, reads=[s4_r], writes=[s8_r])
                                vtt(s16[:, 15:W], s8[:, 15:W], s8[:, 7:W - 8], ALU.add, reads=[s8_r], writes=[s16_r])
                                lev = [(s8, s8_r, 8), (s16, s16_r, 16)]
                            pl, pl_r = pooled[i], pooled_r[i]
                            for half, (sv, sv_r, win) in enumerate(lev):
                                psl = slice(half * 64, (half + 1) * 64)
                                vstt(pl[psl, :], sv[psl, 16:W], 1.0 / win, A[psl, 16:W], ALU.mult, ALU.subtract,
                                     reads=[sv_r, A_r], writes=[pl_r])
                                if tt == 0 and win > 1:
                                    nfx = win - 1
                                    vtt(pfix[psl, 0:nfx], sv[psl, 16:16 + nfx], rcfix[psl, 0:nfx], ALU.mult,
                                        reads=[sv_r, cf_r], writes=[pfix_r])
                                    vtt(pl[psl, 0:nfx], pfix[psl, 0:nfx], A[psl, 16:16 + nfx], ALU.subtract,
                                        reads=[pfix_r, A_r], writes=[pl_r])
                            vcopy(A[:, 0:16], A[:, TT:TT + 16], reads=[A_r], writes=[A_r])
                            ps2, ps2_r = psum()
                            mm(ps2[:, :], pwbd[:, i, :], pl[:, :], True, True, reads=[pwbd_r, pl_r], writes=[ps2_r])
                            vts(ypl[:, i, :], ps2[:, :], pr[:, P_PB + i:P_PB + i + 1], pr[:, P_PS + i:P_PS + i + 1],
                                ALU.add, ALU.mult, reads=[ps2_r, pr_r], writes=[ypl_r[i]])


                    ee, ee_r = smalls["ee"]
                    dtt, dtt_r = smalls["dtt"]
                    dA, dA_r = smalls["dA"]
                    acum, acum_r = smalls["acum"]
                    tot, tot_r = smalls["tot"]
                    cd, cd_r = smalls["cd"]
                    dd, dd_r = smalls["dd"]
                    dte, dte_r = smalls["dte"]
                    xsc, xsc_r = smalls["xsc"]
                    act(ee[:], dtr[:], AF.Exp, reads=[dtr_r], writes=[ee_r])
                    act(dtt[:], ee[:], AF.Ln, reads=[ee_r], writes=[dtt_r], bias=1.0)
                    vtt(dA[:], dtt[:], bc_mid(arep[:, :], NBT), ALU.mult, reads=[dtt_r, arep_r], writes=[dA_r])
                    ps, ps_r = psum()
                    dA2 = dA[:].rearrange("p b h -> p (b h)")
                    mm(ps[:, 0:NBT * 8], tri_f, dA2, True, True, reads=[cf_r, dA_r], writes=[ps_r])
                    mm(ps[:, 64:64 + NBT * 8], ones_f, dA2, True, True, reads=[cf_r, dA_r], writes=[ps_r])
                    vcopy(acum[:].rearrange("p b h -> p (b h)"), ps[:, 0:NBT * 8], reads=[ps_r], writes=[acum_r])
                    vcopy(tot[:].rearrange("p b h -> p (b h)"), ps[:, 64:64 + NBT * 8], reads=[ps_r], writes=[tot_r])
                    act(cd[:], tot[:], AF.Exp, reads=[tot_r], writes=[cd_r])
                    vtt(dd[:], tot[:], acum[:], ALU.subtract, reads=[tot_r, acum_r], writes=[dd_r])
                    act(dte[:], dd[:], AF.Exp, reads=[dd_r], writes=[dte_r])
                    vtt(xsc[:], dte[:], dtt[:], ALU.mult, reads=[dte_r, dtt_r], writes=[xsc_r])
                K.barrier()
                mark('P2 l%d t%d' % (l, tt))
                with ExitStack() as ph:
                    tb = lambda name, shape, dt: ph.enter_context(nc.sbuf_tensor(uname(name), shape, dt))
                    Rm = tb("Rm", [128, 8, 128], F32); Rm_r = R()
                    seg = [tb("seg%d" % i, [128, 4, 128], F32) for i in range(2)]; seg_r = R(2)
                    eex = [tb("eex%d" % i, [128, 4, 128], F32) for i in range(2)]; eex_r = R(2)
                    Mm = [tb("Mm%d" % i, [128, 4, 128], BF16) for i in range(2)]; Mm_r = R(2)
                    Eh = [tb("Eh%d" % i, [128, 4, 128], BF16) for i in range(2)]; Eh_r = R(2)
                    Cs = [tb("Cs%d" % i, [128, 4, 128], BF16) for i in range(2)]; Cs_r = R(2)
                    cbm = tb("cbm", [128, 2, 128], F32); cbm_r = R()
                    Xs = tb("Xs", [128, 8, 64], BF16); Xs_r = R()
                    Xd = tb("Xd", [128, 8, 64], BF16); Xd_r = R()
                    ptmp = tb("ptmp", [128, 8, 64], F32); ptmp_r = R()
                    yg = tb("yg", [128, 4, TT], F32); yg_r = R(4)
                    sqb = tb("sqb2", [128, 4, TT], BF16); sqb_r = R(4)
                    lnv = tb("lnv2", [128, TT], F32); lnv_r = R()
                    rstd = tb("rstd2", [128, TT], F32); rstd_r = R()
                    for blk in range(NBT):
                        bsl = slice(blk * 128, (blk + 1) * 128)
                        vtt(Rm[:], bc_mid(tri_f, 8), bc_last(dA[:, blk, :], 128), ALU.mult,
                            reads=[cf_r, dA_r], writes=[Rm_r])
                        psA, psA_r = psum()
                        psB, psB_r = psum()
                        mm(psA[:, :], ones_f, Rm[:, 0:4, :].rearrange("p a b -> p (a b)"), True, True,
                           reads=[cf_r, Rm_r], writes=[psA_r])
                        mm(psB[:, :], ones_f, Rm[:, 4:8, :].rearrange("p a b -> p (a b)"), True, True,
                           reads=[cf_r, Rm_r], writes=[psB_r])
                        psC, psC_r = psum()
                        for g in range(2):
                            mm(psC[:, g * 128:(g + 1) * 128], BTt[:, g, bsl], CTt[:, g, bsl], True, True,
                               reads=[BT_r[g], CT_r[g]], writes=[psC_r])
                        vtt(cbm[:], psC[:, 0:256].rearrange("p (g l) -> p g l", g=2), bc_mid(tri_f, 2), ALU.mult,
                            reads=[psC_r, cf_r], writes=[cbm_r])
                        for g in range(2):
                            pbc, pbc_r = (psA, psA_r) if g == 0 else (psB, psB_r)
                            pbc3 = pbc[:, :].rearrange("p (k l) -> p k l", k=4)
                            vtt(seg[g][:], pbc3, bc_last(acum[:, blk, 4 * g:4 * g + 4], 128), ALU.subtract,
                                reads=[pbc_r, acum_r], writes=[seg_r[g]])
                            act(eex[g][:], seg[g][:], AF.Exp, reads=[seg_r[g]], writes=[eex_r[g]])
                            vstt(Mm[g][:], eex[g][:], 1.0, bc_mid(cbm[:, g, :], 4), ALU.min, ALU.mult,
                                 reads=[eex_r[g], cbm_r], writes=[Mm_r[g]])
                            act(Eh[g][:], pbc3, AF.Exp, reads=[pbc_r], writes=[Eh_r[g]])
                            vtt(Cs[g][:], Eh[g][:], bc_mid(CTt[:, g, bsl], 4), ALU.mult,
                                reads=[Eh_r[g], CT_r[g]], writes=[Cs_r[g]])
                        xs3 = xstok[:, blk, :].rearrange("p (h d) -> p h d", h=8)
                        vtt(Xs[:], xs3, bc_last(dtt[:, blk, :], 64), ALU.mult,
                            reads=[xstok_r[blk], dtt_r], writes=[Xs_r])
                        vtt(Xd[:], xs3, bc_last(xsc[:, blk, :], 64), ALU.mult,
                            reads=[xstok_r[blk], xsc_r], writes=[Xd_r])
                        psY, psY_r = psum()
                        for h in range(8):
                            g, k = h // 4, h % 4
                            o = psY[(h % 2) * 64:(h % 2 + 1) * 64, (h // 2) * 128:(h // 2 + 1) * 128]
                            mm(o, Xs[:, h, :], Mm[g][:, k, :], True, False, reads=[Xs_r, Mm_r[g]], writes=[psY_r])
                            mm(o, xstok[:, blk, h * 64:(h + 1) * 64], DI[:, h, :], False, False,
                               reads=[xstok_r[blk], DI_r], writes=[psY_r])
                            mm(o, prevbf[:, h, :], Cs[g][:, k, :], False, True, reads=[prevbf_r, Cs_r[g]], writes=[psY_r])
                        vtt(yg[:, :, bsl], psY[:, :].rearrange("p (c l) -> p c l", c=4), zs[:, :, bsl], ALU.mult,
                            reads=[psY_r] + zs_r, writes=yg_r)
                        psS, psS_r = psum()
                        for g in range(2):
                            mm(psS[:, g * 256:(g + 1) * 256], Btok[:, blk, g * 128:(g + 1) * 128],
                               Xd[:, 4 * g:4 * g + 4, :].rearrange("p a b -> p (a b)"), True, True,
                               reads=[Btok_r[blk], Xd_r], writes=[psS_r])
                        vtt(ptmp[:], prev32[:], bc_last(cd[:, blk, :], 64), ALU.mult,
                            reads=[prev32_r, cd_r], writes=[ptmp_r])
                        vtt(prev32[:], ptmp[:], psS[:, :].rearrange("p (h d) -> p h d", h=8), ALU.add,
                            reads=[ptmp_r, psS_r], writes=[prev32_r])
                        act(prevbf[:], prev32[:], AF.Copy, reads=[prev32_r], writes=[prevbf_r])
                    rmsnorm_tile([yg[:, c, :] for c in range(4)], yg_r, 4,
                                 [pr[:, P_SNW + c:P_SNW + c + 1] for c in range(4)], pr_r,
                                 lambda c: zs[:, c, :], zs_r, 512, (sqb, sqb_r, lnv, lnv_r, rstd, rstd_r))
                K.barrier()
                mark('P3 l%d t%d' % (l, tt))
                with ExitStack() as ph:
                    tb = lambda name, shape, dt: ph.enter_context(nc.sbuf_tensor(uname(name), shape, dt))
                    ez = [tb("ez%d" % i, [128, TT], F32) for i in range(2)]; ez_r = R(2)
                    spb = [tb("spb%d" % i, [128, TT], BF16) for i in range(3)]; spb_r = R(3)
                    wTt = [tb("wTt%d" % i, [128, TT], BF16) for i in range(2)]; wT_r = R(2)
                    S32 = tb("S32", [128, TT], F32); S32_r = R()
                    Sbf = [tb("Sbf%d" % i, [128, TT], BF16) for i in range(3)]; Sbf_r = R(3)
                    nkb = NBT * (tt + 1)
                    items = []
                    for c in range(2):
                        for hh in range(2):
                            for i, kb in enumerate(range(nkb - 1, -1, -1)):
                                items.append((c, hh, i, kb))
                    psO_cur = {}

                    def stageA(n):
                        c, hh, i, kb = items[n]
                        po = slice(hh * 64, (hh + 1) * 64)
                        ksl = slice(kb * 128, (kb + 1) * 128)
                        ktt = kb // NBT
                        diag = kb >= NBT * tt
                        mi = kb - NBT * tt
                        e2, s3 = n % 2, n % 3
                        psZ, psZ_r = psum()
                        mm(psZ[:, :], kT[po, c, ksl], qT[po, c, :], True, True,
                           reads=[kT_r[c][ktt], qT_r[c]], writes=[psZ_r])
                        act(ez[e2][:, :], psZ[:, :], AF.Exp, reads=[psZ_r], writes=[ez_r[e2]])
                        act(spb[s3][:, :], ez[e2][:, :], AF.Ln, reads=[ez_r[e2]], writes=[spb_r[s3]], bias=1.0)
                        if diag:
                            vtt(spb[s3][:, :], spb[s3][:, :], sbmask[mi], ALU.mult,
                                reads=[spb_r[s3], cb_r], writes=[spb_r[s3]])
                        if kb > 0:
                            if i == 0:
                                vcopy(S32[:, :], spb[s3][:, :], reads=[spb_r[s3]], writes=[S32_r])
                            else:
                                vtt(S32[:, :], S32[:, :], spb[s3][:, :], ALU.add,
                                    reads=[S32_r, spb_r[s3]], writes=[S32_r])
                            vcopy(Sbf[(n + 1) % 3][:, :], S32[:, :], reads=[S32_r], writes=[Sbf_r[(n + 1) % 3]])

                    def stageB(n):
                        c, hh, i, kb = items[n]
                        hd = 2 * c + hh
                        po = slice(hh * 64, (hh + 1) * 64)
                        ksl = slice(kb * 128, (kb + 1) * 128)
                        ktt = kb // NBT
                        diag = kb >= NBT * tt
                        mi = kb - NBT * tt
                        e2, s3 = n % 2, n % 3
                        if hh == 0 and i == 0:
                            psO_cur[c] = psum(hold=True)
                        psO, psO_r = psO_cur[c]
                        psL, psL_r = psum()
                        mm(psL[:, :], kT[po, c, ksl], qT[po, c, :], True, False,
                           reads=[kT_r[c][ktt], qT_r[c]], writes=[psL_r])
                        mm(psL[:, :], negtri_b, spb[s3][:, :], False, i == 0,
                           reads=[cb_r, spb_r[s3]], writes=[psL_r])
                        if i > 0:
                            mm(psL[:, :], negones_b, Sbf[n % 3][:, :], False, True,
                               reads=[cb_r, Sbf_r[n % 3]], writes=[psL_r])
                        act(wTt[e2][:, :], psL[:, :], AF.Exp, reads=[psL_r], writes=[wT_r[e2]])
                        if diag:
                            vtt(wTt[e2][:, :], wTt[e2][:, :], sbmask[mi], ALU.mult,
                                reads=[wT_r[e2], cb_r], writes=[wT_r[e2]])
                        mm(psO[po, :], vtok[:, kb, hd * 64:(hd + 1) * 64], wTt[e2][:, :], i == 0, kb == 0,
                           reads=[vtok_r[kb], wT_r[e2]], writes=[psO_r])
                        if hh == 1 and kb == 0:
                            act(ysb[:, c, :], psO[:, :], AF.Copy, reads=[psO_r], writes=[ysb_r[c]])
                            psum_release(psO)

                    for n in range(len(items)):
                        stageA(n)
                        if n > 0:
                            stageB(n - 1)
                    stageB(len(items) - 1)
                K.barrier()
                mark('P4 l%d t%d' % (l, tt))
                ycat = [(zs[:, c, :], zs_r[c]) for c in range(4)] + [(ysb[:, c, :], ysb_r[c]) for c in range(2)] + \
                       [(ypl[:, c, :], ypl_r[c]) for c in range(2)]
                if debug and l == 0:
                    dby_v = dby_d.rearrange("(c p) t -> p c t", p=128)
                    for kc in range(8):
                        K.dma("pool", dby_v[:, kc, tsl], ycat[kc][0], reads=[ycat[kc][1]], writes=[dbg_res])
                for oc in range(8):
                    wa, wa_r = SA.get()
                    ps, ps_r = psum()
                    for kc in range(8):
                        mm(ps[:, :], wa[:, kc, :], ycat[kc][0], kc == 0, kc == 7,
                           reads=[wa_r, ycat[kc][1]], writes=[ps_r])
                    vtt(xT[:, oc, tsl], xT[:, oc, tsl], ps[:, :], ALU.add,
                        reads=[xT_r[oc][tt], ps_r], writes=[xT_r[oc][tt]])
                if debug and l == 0:
                    dbx_v = dbx_d.rearrange("(c p) t -> p c t", p=128)
                    for oc in range(8):
                        K.dma("sp", dbx_v[:, oc, tsl], xT[:, oc, tsl], reads=[xT_r[oc][tt]], writes=[dbg_res])
                mark('P5 l%d t%d' % (l, tt))
                with ExitStack() as ph:
                    tb = lambda name, shape, dt: ph.enter_context(nc.sbuf_tensor(uname(name), shape, dt))
                    sqb = tb("sqb3", [128, 8, TT], BF16); sqb_r = R(8)
                    lnv = tb("lnv3", [128, TT], F32); lnv_r = R()
                    rstd = tb("rstd3", [128, TT], F32); rstd_r = R()
                    aT = tb("aT", [128, NFC, TT], BF16); aT_r = R(NFC)
                    sg = [tb("sg%d" % i, [128, TT], F32) for i in range(2)]; sg_r = R(2)
                    rmsnorm_tile([xT[:, c, tsl] for c in range(8)], [xT_r[c][tt] for c in range(8)], 8,
                                 [pr[:, P_N2W + c:P_N2W + c + 1] for c in range(8)], pr_r,
                                 lambda c: hT[:, c, :], hT_r, D, (sqb, sqb_r, lnv, lnv_r, rstd, rstd_r))
                    for fc in range(NFC):
                        wg, wg_r = SA.get()
                        wu, wu_r = SA.get()
                        psG, psG_r = psum()
                        psU, psU_r = psum()
                        for kc in range(8):
                            mm(psG[:, :], wg[:, kc, :], hT[:, kc, :], kc == 0, kc == 7, reads=[wg_r, hT_r[kc]], writes=[psG_r])
                        for kc in range(8):
                            mm(psU[:, :], wu[:, kc, :], hT[:, kc, :], kc == 0, kc == 7, reads=[wu_r, hT_r[kc]], writes=[psU_r])
                        b2 = fc % 2
                        act(sg[b2][:, :], psG[:, :], AF.Silu, reads=[psG_r], writes=[sg_r[b2]])
                        vtt(aT[:, fc, :], sg[b2][:, :], psU[:, :], ALU.mult, reads=[sg_r[b2], psU_r], writes=[aT_r[fc]])
                    for oc in range(8):
                        wd, wd_r = SD.get()
                        ps, ps_r = psum()
                        for fc in range(NFC):
                            mm(ps[:, :], wd[:, fc, :], aT[:, fc, :], fc == 0, fc == NFC - 1,
                               reads=[wd_r, aT_r[fc]], writes=[ps_r])
                        vtt(xT[:, oc, tsl], xT[:, oc, tsl], ps[:, :], ALU.add,
                            reads=[xT_r[oc][tt], ps_r], writes=[xT_r[oc][tt]])
                    if l == depth - 1:
                        ost = [tb("ost%d" % i, [128, 4, TT], F32) for i in range(2)]; ost_r = [R(4), R(4)]
                        out_v = out_d.rearrange("(c p) t -> p c t", p=128)
                        rmsnorm_tile([xT[:, c, tsl] for c in range(8)], [xT_r[c][tt] for c in range(8)], 8,
                                     [fnw[:, c:c + 1] for c in range(8)], fnw_r,
                                     lambda c: ost[c // 4][:, c % 4, :], ost_r[0] + ost_r[1], D,
                                     (sqb, sqb_r, lnv, lnv_r, rstd, rstd_r))
                        for c in range(8):
                            tok = K.dma("sp", out_v[:, c, tsl], ost[c // 4][:, c % 4, :],
                                        reads=[ost_r[c // 4][c % 4]], writes=[out_res[c * NTT + tt]])
                K.barrier()
        mark("END")
        K.final_wait("sp", out_res + [dbg_res])
        nc._marks = marks
    return nc


def _tile_k(w):
    K_, N_ = w.shape
    return np.ascontiguousarray(w.reshape(K_ // 128, 128, N_).transpose(1, 0, 2)).reshape(128, -1)


def _consts():
    j = np.arange(128)[:, None]
    l = np.arange(128)[None, :]
    cf = np.zeros((128, CF_N), np.float32)
    cf[:, CF_IDENT:CF_IDENT + 128] = (j == l)
    cf[:, CF_ONES:CF_ONES + 128] = 1.0
    cf[:, CF_TRI:CF_TRI + 128] = (j <= l)
    cf[:, CF_RCFIX:CF_RCFIX + 16] = 1.0 / (np.arange(16)[None, :] + 1.0)
    cb = np.zeros((128, CB_N), np.float32)
    cb[:, CB_IDENT:CB_IDENT + 128] = (j == l)
    cb[:, CB_ONES:CB_ONES + 128] = 1.0
    cb[:, CB_NEGTRI:CB_NEGTRI + 128] = -1.0 * (j >= l)
    cb[:, CB_NEGONES:CB_NEGONES + 128] = -1.0
    t = np.arange(512)[None, :]
    for i in range(4):
        cb[:, CB_MASK + i * 512:CB_MASK + (i + 1) * 512] = ((128 * i + j) < t)
    return cf, cb


def prep_weights(inp, depth):
    f = lambda a: np.asarray(a, dtype=np.float32)
    w_in, w_out = f(inp["w_in"]), f(inp["w_out"])
    w_gate, w_up, w_down = f(inp["w_gate"]), f(inp["w_up"]), f(inp["w_down"])
    fm_cols = [c * 128 for c in range(4)] + [512 + c * 128 for c in range(8)] + [1544, 1672, 1800, 1928, 2312, 2440]
    w_fm = np.empty((depth, 18, 128, 1024), np.float32)
    w_tm = np.empty((depth, 128, 8 * 264), np.float32)
    w_o = np.empty((depth, 8, 128, 1024), np.float32)
    w_gu = np.empty((depth, NFC, 2, 128, 1024), np.float32)
    w_dn = np.empty((depth, 8, 128, NFC * 128), np.float32)
    par = np.zeros((depth, 128, NPAR), np.float32)
    pw = np.zeros((depth, 128, 256), np.float32)
    for l in range(depth):
        for i, c0 in enumerate(fm_cols):
            w_fm[l, i] = _tile_k(w_in[l][:, c0:c0 + 128])
        w_tm[l] = _tile_k(np.concatenate([w_in[l][:, 2056:2312], w_in[l][:, 1536:1544]], axis=1))
        for oc in range(8):
            w_o[l, oc] = _tile_k(w_out[l][:, oc * 128:(oc + 1) * 128])
            w_dn[l, oc] = _tile_k(w_down[l][:, oc * 128:(oc + 1) * 128])
        for fc in range(NFC):
            w_gu[l, fc, 0] = _tile_k(w_gate[l][:, fc * 128:(fc + 1) * 128])
            w_gu[l, fc, 1] = _tile_k(w_up[l][:, fc * 128:(fc + 1) * 128])
        cm = lambda v: np.asarray(v, np.float32).reshape(-1, 128).T
        par[l, :, P_N1W:P_N1W + 8] = cm(inp["norm1_w"][l])
        par[l, :, P_N2W:P_N2W + 8] = cm(inp["norm2_w"][l])
        cw = f(inp["conv_w"][l])
        for i in range(4):
            par[l, :, P_CW + i:P_CW + 32:4] = cm(cw[i])
        par[l, :, P_CB:P_CB + 8] = cm(inp["conv_b"][l])
        par[l, :, P_SNW:P_SNW + 4] = cm(inp["ssd_norm_w"][l])
        par[l, :, P_PB:P_PB + 2] = cm(f(inp["pool_b"][l]).reshape(-1))
        par[l, :, P_PS:P_PS + 2] = cm(inp["pool_scale"][l])
        par[l, :, P_DTB:P_DTB + 8] = f(inp["dt_bias"][l])[None, :]
        par[l, :, P_ALOG:P_ALOG + 8] = f(inp["a_log"][l])[None, :]
        par[l, :, P_DSK:P_DSK + 8] = f(inp["d_skip"][l])[None, :]
        pwl = f(inp["pool_w"][l])
        for i in range(2):
            for hh in range(2):
                pw[l, hh * 64:(hh + 1) * 64, i * 128 + hh * 64:i * 128 + (hh + 1) * 64] = pwl[2 * i + hh]
    cf, cb = _consts()
    fnw = np.ascontiguousarray(f(inp["final_norm_w"]).reshape(8, 128).T)
    return dict(w_fm=w_fm, w_tm=w_tm, w_o=w_o, w_gu=w_gu, w_dn=w_dn, par=par, pw=pw, fnw=fnw, cf=cf, cb=cb)


_PROG = {}


def run(inputs, depth=DEPTH, cores=NCORES, debug=False):
    if (depth, debug) not in _PROG:
        _PROG[(depth, debug)] = build_program(depth, debug)
    nc = _PROG[(depth, debug)]
    wts = prep_weights(inputs, depth)
    x = np.asarray(inputs["x"], dtype=np.float32)
    in_maps = []
    for b in range(cores):
        m = dict(wts)
        m["xT"] = np.ascontiguousarray(x[b].T)
        in_maps.append(m)
    res = run_bass_kernel_spmd(nc, in_maps, core_ids=list(range(cores)))
    out = np.stack([np.ascontiguousarray(r["outT"].T) for r in res.results], axis=0)
    if debug:
        return out.astype(np.float32), [dict(r) for r in res.results]
    return out.astype(np.float32)


def kernel(**inputs):
    return run(inputs, DEPTH, NCORES)
```

```python
import numpy as np
from contextlib import ExitStack
import concourse.bass as bass
import concourse.mybir as mybir
from concourse.bass_utils import run_bass_kernel_spmd

F32 = mybir.dt.float32
BF16 = mybir.dt.bfloat16
AF = mybir.ActivationFunctionType
ALU = mybir.AluOpType

D = 1024
T = 2048
DEPTH = 4
NCORES = 8
DFF = 2816
NFC = DFF // 128
TT = 512
NTT = T // TT
NBT = TT // 128
NB = T // 128
EPS = 1e-6
NPAR = 88

P_N1W, P_N2W, P_CW, P_CB, P_SNW, P_PB, P_PS, P_DTB, P_ALOG, P_DSK = 0, 8, 16, 48, 56, 60, 62, 64, 72, 80
CF_IDENT, CF_ONES, CF_TRI, CF_RCFIX, CF_N = 0, 128, 256, 384, 400
CB_IDENT, CB_ONES, CB_NEGTRI, CB_NEGONES, CB_MASK, CB_N = 0, 128, 256, 384, 512, 512 + 4 * 512


class Res:
    __slots__ = ("w", "r")

    def __init__(self):
        self.w = None
        self.r = {}


class Eng:
    def __init__(self, name, h, sem):
        self.name, self.h, self.sem = name, h, sem
        self.count = 0
        self.waited = {}


class KB:
    def __init__(self, nc, stack, n_dma_sems=40):
        self.nc = nc
        self.eng = {}
        for name, h in (("pe", nc.tensor), ("act", nc.scalar), ("dve", nc.vector), ("pool", nc.gpsimd), ("sp", nc.sync)):
            sem = stack.enter_context(nc.semaphore("sem_" + name))
            self.eng[name] = Eng(name, h, sem)
        self.dsem = [stack.enter_context(nc.semaphore("dsem%d" % i)) for i in range(n_dma_sems)]
        self.dval = [0] * n_dma_sems
        self.dnext = 0
        self.semkey = {}

    def _key(self, sem):
        return id(sem)

    def _deps(self, e, reads, writes):
        need = {}

        def add(tok, raw):
            sem, val = tok
            if sem is e.sem and (e.name == "pe" or not raw):
                return
            k = id(sem)
            if k not in need or need[k][1] < val:
                need[k] = (sem, val)

        for r in reads:
            if r.w is not None:
                add(r.w, True)
        for w in writes:
            if w.w is not None:
                add(w.w, False)
            for k, tok in w.r.items():
                add(tok, False)
        for k, (sem, val) in need.items():
            if e.waited.get(k, 0) < val:
                e.h.wait_ge(sem, val)
                e.waited[k] = val

    def _mark(self, tok, reads, writes):
        k = id(tok[0])
        for r in reads:
            r.r[k] = tok
        for w in writes:
            w.w = tok
            w.r = {}

    def op(self, en, fn, reads=(), writes=()):
        e = self.eng[en]
        self._deps(e, reads, writes)
        ins = fn(e.h)
        e.count += 1
        ins.then_inc(e.sem, 1)
        tok = (e.sem, e.count)
        self._mark(tok, reads, writes)
        return tok

    def dma(self, qn, out, in_, reads=(), writes=()):
        q = self.eng[qn]
        slot = self.dnext
        self.dnext = (slot + 1) % len(self.dsem)
        sem = self.dsem[slot]
        self._deps(q, reads, writes)
        if self.dval[slot] > 0 and q.waited.get(id(sem), 0) < self.dval[slot]:
            q.h.wait_ge(sem, self.dval[slot])
            q.waited[id(sem)] = self.dval[slot]
        ins = q.h.dma_start(out=out, in_=in_)
        self.dval[slot] += 16
        ins.then_inc(sem, 16)
        tok = (sem, self.dval[slot])
        self._mark(tok, reads, writes)
        return tok

    def barrier(self):
        toks = [(e.sem, e.count) for e in self.eng.values() if e.count > 0]
        for e in self.eng.values():
            for sem, val in toks:
                if sem is e.sem:
                    continue
                if e.waited.get(id(sem), 0) < val:
                    e.h.wait_ge(sem, val)
                    e.waited[id(sem)] = val

    def final_wait(self, en, reslist):
        e = self.eng[en]
        self._deps(e, reslist, [])


def build_program(depth=DEPTH, debug=False):
    nc = bass.Bass("TRN2", target_bir_lowering=False)
    dt_ = lambda n, s: nc.dram_tensor(n, s, F32, kind="ExternalInput").ap()
    xT_d = dt_("xT", [D, T])
    wfm_d = dt_("w_fm", [depth, 18, 128, 8 * 128])
    wtm_d = dt_("w_tm", [depth, 128, 8 * 264])
    wout_d = dt_("w_o", [depth, 8, 128, 8 * 128])
    wgu_d = dt_("w_gu", [depth, NFC, 2, 128, 8 * 128])
    wdn_d = dt_("w_dn", [depth, 8, 128, NFC * 128])
    par_d = dt_("par", [depth, 128, NPAR])
    pw_d = dt_("pw", [depth, 128, 2 * 128])
    fnw_d = dt_("fnw", [128, 8])
    cf_d = dt_("cf", [128, CF_N])
    cb_d = dt_("cb", [128, CB_N])
    out_d = nc.dram_tensor("outT", [D, T], F32, kind="ExternalOutput").ap()
    if debug:
        dby_d = nc.dram_tensor("dbg_y", [D, T], F32, kind="ExternalOutput").ap()
        dbx_d = nc.dram_tensor("dbg_x", [D, T], F32, kind="ExternalOutput").ap()

    with ExitStack() as st:
        K = KB(nc, st)
        uid = [0]

        def uname(name):
            uid[0] += 1
            return "s_%s_%d" % (name, uid[0])

        sb = lambda name, shape, dt: st.enter_context(nc.sbuf_tensor(uname(name), shape, dt))

        def R(n=None):
            return Res() if n is None else [Res() for _ in range(n)]

        xT = sb("xT", [128, 8, T], F32)
        xT_r = [[Res() for _ in range(NTT)] for _ in range(8)]
        kT = sb("kT", [128, 2, T], BF16)
        kT_r = [[Res() for _ in range(NTT)] for _ in range(2)]
        vtok = sb("vtok", [128, NB, 256], BF16)
        vtok_r = R(NB)
        hT = sb("hT", [128, 8, TT], BF16)
        hT_r = R(8)
        zs = sb("zs", [128, 4, TT], BF16)
        zs_r = R(4)
        ysb = sb("ysb", [128, 2, TT], BF16)
        ysb_r = R(2)
        ypl = sb("ypl", [128, 2, TT], BF16)
        ypl_r = R(2)
        xstok = sb("xstok", [128, NBT, 512], BF16)
        xstok_r = R(NBT)
        BTt = sb("BT", [128, 2, TT], BF16)
        BT_r = R(2)
        Btok = sb("Btok", [128, NBT, 256], BF16)
        Btok_r = R(NBT)
        CTt = sb("CT", [128, 2, TT], BF16)
        CT_r = R(2)
        qT = sb("qT", [128, 2, TT], BF16)
        qT_r = R(2)
        cf = sb("cf", [128, CF_N], F32)
        cf_r = R()
        cb = sb("cb", [128, CB_N], BF16)
        cb_r = R()
        par = [sb("par%d" % i, [128, NPAR], F32) for i in range(2)]
        par_r = R(2)
        fnw = sb("fnw", [128, 8], F32)
        fnw_r = R()
        DI = sb("DI", [128, 8, 128], BF16)
        DI_r = R()
        pwbd = sb("pwbd", [128, 2, 128], BF16)
        pwbd_r = R()
        arep = sb("arep", [128, 8], F32)
        arep_r = R()
        prev32 = sb("prev32", [128, 8, 64], F32)
        prev32_r = R()
        prevbf = sb("prevbf", [128, 8, 64], BF16)
        prevbf_r = R()
        chalo = sb("chalo", [128, 8, 4], F32)
        chalo_r = R(8)
        pst = sb("pst", [128, 2, 16 + TT], F32)
        pst_r = R(2)
        smalls = {}
        for nm in ("dtr", "ee", "dtt", "dA", "acum", "tot", "cd", "dd", "dte", "xsc"):
            smalls[nm] = (sb("sm_" + nm, [128, NBT, 8], F32), Res())
        NA, NTM, ND = 6, 1, 2
        wA = [sb("wA%d" % i, [128, 8, 128], BF16) for i in range(NA)]
        wA_r = R(NA)
        wTm = [sb("wT%d" % i, [128, 8, 264], BF16) for i in range(NTM)]
        wTm_r = R(NTM)
        wD = [sb("wD%d" % i, [128, NFC, 128], BF16) for i in range(ND)]
        wD_r = R(ND)
        psb = [st.enter_context(nc.psum_tensor("ps%d" % i, [128, 512], F32)) for i in range(8)]
        psb_r = R(8)
        psn = [0]

        held = set()

        def psum(hold=False):
            i = psn[0]
            while i in held:
                i = (i + 1) % 8
            psn[0] = (i + 1) % 8
            if hold:
                held.add(i)
            return psb[i], psb_r[i]

        def psum_release(ps):
            for i in range(8):
                if psb[i] is ps:
                    held.discard(i)

        ident_f = cf[:, CF_IDENT:CF_IDENT + 128]
        ones_f = cf[:, CF_ONES:CF_ONES + 128]
        tri_f = cf[:, CF_TRI:CF_TRI + 128]
        rcfix = cf[:, CF_RCFIX:CF_RCFIX + 16]
        ident_b = cb[:, CB_IDENT:CB_IDENT + 128]
        ones_b = cb[:, CB_ONES:CB_ONES + 128]
        negtri_b = cb[:, CB_NEGTRI:CB_NEGTRI + 128]
        negones_b = cb[:, CB_NEGONES:CB_NEGONES + 128]
        sbmask = [cb[:, CB_MASK + i * 512:CB_MASK + (i + 1) * 512] for i in range(4)]

        K.dma("sp", cf[:], cf_d[:], writes=[cf_r])
        K.dma("pool", cb[:], cb_d[:], writes=[cb_r])
        K.dma("sp", fnw[:], fnw_d[:], writes=[fnw_r])
        xT_v = xT_d.rearrange("(c p) t -> p c t", p=128)
        for c in range(8):
            K.dma("sp", xT[:, c, :], xT_v[:, c, :], writes=xT_r[c])

        loadsA, loadsT, loadsD = [], [], []
        for l in range(depth):
            for tt in range(NTT):
                loadsT.append(wtm_d[l])
                for oc in range(18):
                    loadsA.append(wfm_d[l, oc])
                for oc in range(8):
                    loadsA.append(wout_d[l, oc])
                for fc in range(NFC):
                    loadsA.append(wgu_d[l, fc, 0])
                    loadsA.append(wgu_d[l, fc, 1])
                for oc in range(8):
                    loadsD.append(wdn_d[l, oc])

        class Stream:
            def __init__(self, loads, tiles, res, shp):
                self.loads, self.tiles, self.res, self.shp = loads, tiles, res, shp
                self.issued = 0
                self.used = 0

            def get(self):
                i = self.used
                n = len(self.tiles)
                while self.issued < len(self.loads) and self.issued <= i + n - self.shp:
                    j = self.issued
                    tl = self.tiles[j % n]
                    K.dma("pool", tl[:].rearrange("p a b -> p (a b)"), self.loads[j], writes=[self.res[j % n]])
                    self.issued += 1
                self.used += 1
                return self.tiles[i % n], self.res[i % n]

        SA = Stream(loadsA, wA, wA_r, 2)
        STm = Stream(loadsT, wTm, wTm_r, 1)
        SD = Stream(loadsD, wD, wD_r, 1)

        def mm(out, lhsT, rhs, start, stop, reads, writes):
            return K.op("pe", lambda h: h.matmul(out, lhsT, rhs, start=start, stop=stop), reads=reads, writes=writes)

        def act(out, in_, func, reads, writes, bias=None, scale=None):
            kw = {}
            if bias is not None:
                kw["bias"] = bias
            if scale is not None:
                kw["scale"] = scale
            return K.op("act", lambda h: h.activation(out=out, in_=in_, func=func, **kw), reads=reads, writes=writes)

        def vtt(out, in0, in1, op, reads, writes, en="dve"):
            return K.op(en, lambda h: h.tensor_tensor(out, in0, in1, op), reads=reads, writes=writes)

        def vts(out, in0, s1, s2, op0, op1, reads, writes, en="dve"):
            if s2 is None:
                return K.op(en, lambda h: h.tensor_scalar(out, in0, s1, None, op0), reads=reads, writes=writes)
            return K.op(en, lambda h: h.tensor_scalar(out, in0, s1, s2, op0, op1), reads=reads, writes=writes)

        def vstt(out, in0, sc, in1, op0, op1, reads, writes, en="dve"):
            return K.op(en, lambda h: h.scalar_tensor_tensor(out, in0, sc, in1, op0, op1), reads=reads, writes=writes)

        def vcopy(out, in_, reads, writes, en="dve"):
            return K.op(en, lambda h: h.tensor_copy(out, in_), reads=reads, writes=writes)

        def vmemset(ap, val, writes, en="dve"):
            return K.op(en, lambda h: h.memset(ap, val), writes=writes)

        def bc_mid(ap2, n):
            return ap2.unsqueeze(1).broadcast_to([ap2.shape[0], n, ap2.shape[1]])

        def bc_last(ap2, n):
            return ap2.unsqueeze(2).broadcast_to([ap2.shape[0], ap2.shape[1], n])

        dbg_outs = {}
        marks = []

        def mark(name):
            marks.append((name, {n: e.count for n, e in K.eng.items()}))
        out_res = [Res() for _ in range(8 * NTT)]
        dbg_res = Res()

        def rmsnorm_tile(src_chunks, src_res, nchunk, wcols, wres, dst_fn, dst_res, nfeat, tmp):
            sqb, sqb_r, lnv, lnv_r, rstd, rstd_r = tmp
            for c in range(nchunk):
                act(sqb[:, c, :], src_chunks[c], AF.Square, reads=[src_res[c]], writes=[sqb_r[c]])
            ps, ps_r = psum()
            for c in range(nchunk):
                mm(ps[:, :], ones_b, sqb[:, c, :], c == 0, c == nchunk - 1, reads=[cb_r, sqb_r[c]], writes=[ps_r])
            act(lnv[:, :], ps[:, :], AF.Ln, reads=[ps_r], writes=[lnv_r], bias=EPS, scale=1.0 / nfeat)
            act(rstd[:, :], lnv[:, :], AF.Exp, reads=[lnv_r], writes=[rstd_r], scale=-0.5)
            for c in range(nchunk):
                vstt(dst_fn(c), src_chunks[c], wcols[c], rstd[:, :], ALU.mult, ALU.mult,
                     reads=[src_res[c], wres, rstd_r], writes=[dst_res[c]])

        for l in range(depth):
            pr = par[l % 2]
            pr_r = par_r[l % 2]
            K.dma("sp", pr[:], par_d[l], writes=[pr_r])
            K.dma("pool", pwbd[:].rearrange("p a b -> p (a b)"), pw_d[l], writes=[pwbd_r])
            vmemset(prev32[:], 0.0, [prev32_r])
            vmemset(prevbf[:], 0.0, [prevbf_r])
            vmemset(chalo[:], 0.0, chalo_r)
            vmemset(pst[:], 0.0, pst_r)
            act(arep[:, :], pr[:, P_ALOG:P_ALOG + 8], AF.Exp, reads=[pr_r], writes=[arep_r])
            vts(arep[:, :], arep[:, :], -1.0, None, ALU.mult, None, reads=[arep_r], writes=[arep_r])
            for h in range(8):
                vts(DI[:, h, :], ident_f, pr[:, P_DSK + h:P_DSK + h + 1], None, ALU.mult, None,
                    reads=[cf_r, pr_r], writes=[DI_r])

            for tt in range(NTT):
                tsl = slice(tt * TT, (tt + 1) * TT)
                mark('P1 l%d t%d' % (l, tt))
                with ExitStack() as ph:
                    tb = lambda name, shape, dt: ph.enter_context(nc.sbuf_tensor(uname(name), shape, dt))
                    sqb = tb("sqb", [128, 8, TT], BF16); sqb_r = R(8)
                    lnv = tb("lnv", [128, TT], F32); lnv_r = R()
                    rstd = tb("rstd", [128, TT], F32); rstd_r = R()
                    cst = [tb("cst%d" % i, [128, 4 + TT], F32) for i in range(2)]; cst_r = R(2)
                    ctmp = [tb("ctmp%d" % i, [128, TT], F32) for i in range(2)]; ctmp_r = R(2)
                    cact = [tb("cact%d" % i, [128, TT], BF16) for i in range(2)]; cact_r = R(2)
                    s2 = tb("s2", [128, 16 + TT], F32); s2_r = R()
                    s4 = tb("s4", [128, 16 + TT], F32); s4_r = R()
                    s8 = tb("s8", [128, 16 + TT], F32); s8_r = R()
                    s16 = tb("s16", [128, 16 + TT], F32); s16_r = R()
                    pfix = tb("pfix", [128, 16], F32); pfix_r = R()
                    pooled = [tb("pooled%d" % i, [128, TT], BF16) for i in range(2)]; pooled_r = R(2)
                    dtr, dtr_r = smalls["dtr"]

                    rmsnorm_tile([xT[:, c, tsl] for c in range(8)], [xT_r[c][tt] for c in range(8)], 8,
                                 [pr[:, P_N1W + c:P_N1W + c + 1] for c in range(8)], pr_r,
                                 lambda c: hT[:, c, :], hT_r, D, (sqb, sqb_r, lnv, lnv_r, rstd, rstd_r))

                    wt, wt_r = STm.get()
                    for blk in range(NBT):
                        ps, ps_r = psum()
                        bsl = slice(blk * 128, (blk + 1) * 128)
                        for kc in range(8):
                            mm(ps[:, 0:264], hT[:, kc, bsl], wt[:, kc, :], kc == 0, kc == 7,
                               reads=[hT_r[kc], wt_r], writes=[ps_r])
                        gb = tt * NBT + blk
                        act(vtok[:, gb, :], ps[:, 0:256], AF.Copy, reads=[ps_r], writes=[vtok_r[gb]])
                        vtt(dtr[:, blk, :], ps[:, 256:264], pr[:, P_DTB:P_DTB + 8], ALU.add,
                            reads=[ps_r, pr_r], writes=[dtr_r])

                    for oc in range(18):
                        wa, wa_r = SA.get()
                        ps, ps_r = psum()
                        for kc in range(8):
                            mm(ps[:, :], wa[:, kc, :], hT[:, kc, :], kc == 0, kc == 7,
                               reads=[wa_r, hT_r[kc]], writes=[ps_r])
                        if oc < 4:
                            act(zs[:, oc, :], ps[:, :], AF.Silu, reads=[ps_r], writes=[zs_r[oc]])
                        elif oc < 12:
                            j = oc - 4
                            sl_ = j % 2
                            cs, cs_r = cst[sl_], cst_r[sl_]
                            ct, ct_r = ctmp[sl_], ctmp_r[sl_]
                            ca, ca_r = cact[sl_], cact_r[sl_]
                            vcopy(cs[:, 0:3], chalo[:, j, 0:3], reads=[chalo_r[j]], writes=[cs_r])
                            act(cs[:, 3:3 + TT], ps[:, :], AF.Copy, reads=[ps_r], writes=[cs_r])
                            vcopy(chalo[:, j, 0:3], cs[:, TT:TT + 3], reads=[cs_r], writes=[chalo_r[j]])
                            cw = lambda i: pr[:, P_CW + j * 4 + i:P_CW + j * 4 + i + 1]
                            vts(ct[:, :], cs[:, 0:TT], cw(0), pr[:, P_CB + j:P_CB + j + 1], ALU.mult, ALU.add,
                                reads=[cs_r, pr_r], writes=[ct_r])
                            for i in range(1, 4):
                                vstt(ct[:, :], cs[:, i:i + TT], cw(i), ct[:, :], ALU.mult, ALU.add,
                                     reads=[cs_r, pr_r, ct_r], writes=[ct_r])
                            if j < 4:
                                act(ca[:, :], ct[:, :], AF.Silu, reads=[ct_r], writes=[ca_r])
                                src, src_r = ca, ca_r
                            elif j < 6:
                                g = j - 4
                                act(BTt[:, g, :], ct[:, :], AF.Silu, reads=[ct_r], writes=[BT_r[g]])
                                src, src_r = None, BT_r[g]
                            else:
                                g = j - 6
                                act(CTt[:, g, :], ct[:, :], AF.Silu, reads=[ct_r], writes=[CT_r[g]])
                            if j < 6:
                                pt, pt_r = psum()
                                ptb = pt[:, :].bitcast(BF16)
                                for blk in range(NBT):
                                    bsl = slice(blk * 128, (blk + 1) * 128)
                                    srcap = ca[:, bsl] if j < 4 else BTt[:, j - 4, bsl]
                                    K.op("pe", lambda h, o=ptb[:, blk * 128:(blk + 1) * 128], s=srcap: h.transpose(o, s, ident_b),
                                         reads=[src_r, cb_r], writes=[pt_r])
                                if j < 4:
                                    vcopy(xstok[:, :, j * 128:(j + 1) * 128],
                                          ptb[:, 0:NBT * 128].rearrange("p (b c) -> p b c", b=NBT),
                                          reads=[pt_r], writes=xstok_r)
                                else:
                                    g = j - 4
                                    vcopy(Btok[:, :, g * 128:(g + 1) * 128],
                                          ptb[:, 0:NBT * 128].rearrange("p (b c) -> p b c", b=NBT),
                                          reads=[pt_r], writes=Btok_r)
                        elif oc < 14:
                            i = oc - 12
                            K.op('act', lambda h, o=qT[:, i, :], s=ps[:, :]: h.mul(o, s, 0.125), reads=[ps_r], writes=[qT_r[i]])
                        elif oc < 16:
                            i = oc - 14
                            act(kT[:, i, tsl], ps[:, :], AF.Copy, reads=[ps_r], writes=[kT_r[i][tt]])
                        else:
                            i = oc - 16
                            A = pst[:, i, :]
                            A_r = pst_r[i]
                            W = 16 + TT
                            act(A[:, 16:W], ps[:, :], AF.Copy, reads=[ps_r], writes=[A_r])
                            vtt(s2[:, 1:W], A[:, 1:W], A[:, 0:W - 1], ALU.add, reads=[A_r], writes=[s2_r])
                            vtt(s4[:, 3:W], s2[:, 3:W], s2[:, 1:W - 2], ALU.add, reads=[s2_r], writes=[s4_r])
                            if i == 0:
                                lev = [(s2, s2_r, 2), (s4, s4_r, 4)]
                            else:
                                vtt(s8[:, 7:W], s4[:, 7:W], s4[:, 3:W - 4], ALU.add, reads=[s4_r], writes=[s8_r])
                                vtt(s16[:, 15:W], s8[:, 15:W], s8[:, 7:W - 8], ALU.add, reads=[s8_r], writes=[s16_r])
                                lev = [(s8, s8_r, 8), (s16, s16_r, 16)]
                            pl, pl_r = pooled[i], pooled_r[i]
                            for half, (sv, sv_r, win) in enumerate(lev):
                                psl = slice(half * 64, (half + 1) * 64)
                                vstt(pl[psl, :], sv[psl, 16:W], 1.0 / win, A[psl, 16:W], ALU.mult, ALU.subtract,
                                     reads=[sv_r, A_r], writes=[pl_r])
                                if tt == 0 and win > 1:
                                    nfx = win - 1
                                    vtt(pfix[psl, 0:nfx], sv[psl, 16:16 + nfx], rcfix[psl, 0:nfx], ALU.mult,
                                        reads=[sv_r, cf_r], writes=[pfix_r])
                                    vtt(pl[psl, 0:nfx], pfix[psl, 0:nfx], A[psl, 16:16 + nfx], ALU.subtract,
                                        reads=[pfix_r, A_r], writes=[pl_r])
                            vcopy(A[:, 0:16], A[:, TT:TT + 16], reads=[A_r], writes=[A_r])
                            ps2, ps2_r = psum()
                            mm(ps2[:, :], pwbd[:, i, :], pl[:, :], True, True, reads=[pwbd_r, pl_r], writes=[ps2_r])
                            vts(ypl[:, i, :], ps2[:, :], pr[:, P_PB + i:P_PB + i + 1], pr[:, P_PS + i:P_PS + i + 1],
                                ALU.add, ALU.mult, reads=[ps2_r, pr_r], writes=[ypl_r[i]])


                    ee, ee_r = smalls["ee"]
                    dtt, dtt_r = smalls["dtt"]
                    dA, dA_r = smalls["dA"]
                    acum, acum_r = smalls["acum"]
                    tot, tot_r = smalls["tot"]
                    cd, cd_r = smalls["cd"]
                    dd, dd_r = smalls["dd"]
                    dte, dte_r = smalls["dte"]
                    xsc, xsc_r = smalls["xsc"]
                    act(ee[:], dtr[:], AF.Exp, reads=[dtr_r], writes=[ee_r])
                    act(dtt[:], ee[:], AF.Ln, reads=[ee_r], writes=[dtt_r], bias=1.0)
                    vtt(dA[:], dtt[:], bc_mid(arep[:, :], NBT), ALU.mult, reads=[dtt_r, arep_r], writes=[dA_r])
                    ps, ps_r = psum()
                    dA2 = dA[:].rearrange("p b h -> p (b h)")
                    mm(ps[:, 0:NBT * 8], tri_f, dA2, True, True, reads=[cf_r, dA_r], writes=[ps_r])
                    mm(ps[:, 64:64 + NBT * 8], ones_f, dA2, True, True, reads=[cf_r, dA_r], writes=[ps_r])
                    vcopy(acum[:].rearrange("p b h -> p (b h)"), ps[:, 0:NBT * 8], reads=[ps_r], writes=[acum_r])
                    vcopy(tot[:].rearrange("p b h -> p (b h)"), ps[:, 64:64 + NBT * 8], reads=[ps_r], writes=[tot_r])
                    act(cd[:], tot[:], AF.Exp, reads=[tot_r], writes=[cd_r])
                    vtt(dd[:], tot[:], acum[:], ALU.subtract, reads=[tot_r, acum_r], writes=[dd_r])
                    act(dte[:], dd[:], AF.Exp, reads=[dd_r], writes=[dte_r])
                    vtt(xsc[:], dte[:], dtt[:], ALU.mult, reads=[dte_r, dtt_r], writes=[xsc_r])
                K.barrier()
                mark('P2 l%d t%d' % (l, tt))
                with ExitStack() as ph:
                    tb = lambda name, shape, dt: ph.enter_context(nc.sbuf_tensor(uname(name), shape, dt))
                    Rm = tb("Rm", [128, 8, 128], F32); Rm_r = R()
                    seg = [tb("seg%d" % i, [128, 4, 128], F32) for i in range(2)]; seg_r = R(2)
                    eex = [tb("eex%d" % i, [128, 4, 128], F32) for i in range(2)]; eex_r = R(2)
                    Mm = [tb("Mm%d" % i, [128, 4, 128], BF16) for i in range(2)]; Mm_r = R(2)
                    Eh = [tb("Eh%d" % i, [128, 4, 128], BF16) for i in range(2)]; Eh_r = R(2)
                    Cs = [tb("Cs%d" % i, [128, 4, 128], BF16) for i in range(2)]; Cs_r = R(2)
                    cbm = tb("cbm", [128, 2, 128], F32); cbm_r = R()
                    Xs = tb("Xs", [128, 8, 64], BF16); Xs_r = R()
                    Xd = tb("Xd", [128, 8, 64], BF16); Xd_r = R()
                    ptmp = tb("ptmp", [128, 8, 64], F32); ptmp_r = R()
                    yg = tb("yg", [128, 4, TT], F32); yg_r = R(4)
                    sqb = tb("sqb2", [128, 4, TT], BF16); sqb_r = R(4)
                    lnv = tb("lnv2", [128, TT], F32); lnv_r = R()
                    rstd = tb("rstd2", [128, TT], F32); rstd_r = R()
                    for blk in range(NBT):
                        bsl = slice(blk * 128, (blk + 1) * 128)
                        vtt(Rm[:], bc_mid(tri_f, 8), bc_last(dA[:, blk, :], 128), ALU.mult,
                            reads=[cf_r, dA_r], writes=[Rm_r])
                        psA, psA_r = psum()
                        psB, psB_r = psum()
                        mm(psA[:, :], ones_f, Rm[:, 0:4, :].rearrange("p a b -> p (a b)"), True, True,
                           reads=[cf_r, Rm_r], writes=[psA_r])
                        mm(psB[:, :], ones_f, Rm[:, 4:8, :].rearrange("p a b -> p (a b)"), True, True,
                           reads=[cf_r, Rm_r], writes=[psB_r])
                        psC, psC_r = psum()
                        for g in range(2):
                            mm(psC[:, g * 128:(g + 1) * 128], BTt[:, g, bsl], CTt[:, g, bsl], True, True,
                               reads=[BT_r[g], CT_r[g]], writes=[psC_r])
                        vtt(cbm[:], psC[:, 0:256].rearrange("p (g l) -> p g l", g=2), bc_mid(tri_f, 2), ALU.mult,
                            reads=[psC_r, cf_r], writes=[cbm_r])
                        for g in range(2):
                            pbc, pbc_r = (psA, psA_r) if g == 0 else (psB, psB_r)
                            pbc3 = pbc[:, :].rearrange("p (k l) -> p k l", k=4)
                            vtt(seg[g][:], pbc3, bc_last(acum[:, blk, 4 * g:4 * g + 4], 128), ALU.subtract,
                                reads=[pbc_r, acum_r], writes=[seg_r[g]])
                            act(eex[g][:], seg[g][:], AF.Exp, reads=[seg_r[g]], writes=[eex_r[g]])
                            vstt(Mm[g][:], eex[g][:], 1.0, bc_mid(cbm[:, g, :], 4), ALU.min, ALU.mult,
                                 reads=[eex_r[g], cbm_r], writes=[Mm_r[g]])
                            act(Eh[g][:], pbc3, AF.Exp, reads=[pbc_r], writes=[Eh_r[g]])
                            vtt(Cs[g][:], Eh[g][:], bc_mid(CTt[:, g, bsl], 4), ALU.mult,
                                reads=[Eh_r[g], CT_r[g]], writes=[Cs_r[g]])
                        xs3 = xstok[:, blk, :].rearrange("p (h d) -> p h d", h=8)
                        vtt(Xs[:], xs3, bc_last(dtt[:, blk, :], 64), ALU.mult,
                            reads=[xstok_r[blk], dtt_r], writes=[Xs_r])
                        vtt(Xd[:], xs3, bc_last(xsc[:, blk, :], 64), ALU.mult,
                            reads=[xstok_r[blk], xsc_r], writes=[Xd_r])
                        psY, psY_r = psum()
                        for h in range(8):
                            g, k = h // 4, h % 4
                            o = psY[(h % 2) * 64:(h % 2 + 1) * 64, (h // 2) * 128:(h // 2 + 1) * 128]
                            mm(o, Xs[:, h, :], Mm[g][:, k, :], True, False, reads=[Xs_r, Mm_r[g]], writes=[psY_r])
                            mm(o, xstok[:, blk, h * 64:(h + 1) * 64], DI[:, h, :], False, False,
                               reads=[xstok_r[blk], DI_r], writes=[psY_r])
                            mm(o, prevbf[:, h, :], Cs[g][:, k, :], False, True, reads=[prevbf_r, Cs_r[g]], writes=[psY_r])
                        vtt(yg[:, :, bsl], psY[:, :].rearrange("p (c l) -> p c l", c=4), zs[:, :, bsl], ALU.mult,
                            reads=[psY_r] + zs_r, writes=yg_r)
                        psS, psS_r = psum()
                        for g in range(2):
                            mm(psS[:, g * 256:(g + 1) * 256], Btok[:, blk, g * 128:(g + 1) * 128],
                               Xd[:, 4 * g:4 * g + 4, :].rearrange("p a b -> p (a b)"), True, True,
                               reads=[Btok_r[blk], Xd_r], writes=[psS_r])
                        vtt(ptmp[:], prev32[:], bc_last(cd[:, blk, :], 64), ALU.mult,
                            reads=[prev32_r, cd_r], writes=[ptmp_r])
                        vtt(prev32[:], ptmp[:], psS[:, :].rearrange("p (h d) -> p h d", h=8), ALU.add,
                            reads=[ptmp_r, psS_r], writes=[prev32_r])
                        act(prevbf[:], prev32[:], AF.Copy, reads=[prev32_r], writes=[prevbf_r])
                    rmsnorm_tile([yg[:, c, :] for c in range(4)], yg_r, 4,
                                 [pr[:, P_SNW + c:P_SNW + c + 1] for c in range(4)], pr_r,
                                 lambda c: zs[:, c, :], zs_r, 512, (sqb, sqb_r, lnv, lnv_r, rstd, rstd_r))
                K.barrier()
                mark('P3 l%d t%d' % (l, tt))
                with ExitStack() as ph:
                    tb = lambda name, shape, dt: ph.enter_context(nc.sbuf_tensor(uname(name), shape, dt))
                    ez = [tb("ez%d" % i, [128, TT], F32) for i in range(2)]; ez_r = R(2)
                    spb = [tb("spb%d" % i, [128, TT], BF16) for i in range(3)]; spb_r = R(3)
                    wTt = [tb("wTt%d" % i, [128, TT], BF16) for i in range(2)]; wT_r = R(2)
                    S32 = tb("S32", [128, TT], F32); S32_r = R()
                    Sbf = [tb("Sbf%d" % i, [128, TT], BF16) for i in range(3)]; Sbf_r = R(3)
                    nkb = NBT * (tt + 1)
                    items = []
                    for c in range(2):
                        for hh in range(2):
                            for i, kb in enumerate(range(nkb - 1, -1, -1)):
                                items.append((c, hh, i, kb))
                    psO_cur = {}

                    psZs = {}
                    psLs = {}

                    def info(n):
                        c, hh, i, kb = items[n]
                        return (c, hh, i, kb, 2 * c + hh, slice(hh * 64, (hh + 1) * 64), slice(kb * 128, (kb + 1) * 128),
                                kb // NBT, kb >= NBT * tt, kb - NBT * tt)

                    def S1(n):
                        c, hh, i, kb, hd, po, ksl, ktt, diag, mi = info(n)
                        psZ, psZ_r = psum()
                        psZs[n] = (psZ, psZ_r)
                        mm(psZ[:, :], kT[po, c, ksl], qT[po, c, :], True, True,
                           reads=[kT_r[c][ktt], qT_r[c]], writes=[psZ_r])

                    def S2(n):
                        c, hh, i, kb, hd, po, ksl, ktt, diag, mi = info(n)
                        e2, s3 = n % 2, n % 3
                        psZ, psZ_r = psZs.pop(n)
                        act(ez[e2][:, :], psZ[:, :], AF.Exp, reads=[psZ_r], writes=[ez_r[e2]])
                        act(spb[s3][:, :], ez[e2][:, :], AF.Ln, reads=[ez_r[e2]], writes=[spb_r[s3]], bias=1.0)
                        if diag:
                            vtt(spb[s3][:, :], spb[s3][:, :], sbmask[mi], ALU.mult,
                                reads=[spb_r[s3], cb_r], writes=[spb_r[s3]])
                        if kb > 0:
                            if i == 0:
                                vcopy(S32[:, :], spb[s3][:, :], reads=[spb_r[s3]], writes=[S32_r])
                            else:
                                vtt(S32[:, :], S32[:, :], spb[s3][:, :], ALU.add,
                                    reads=[S32_r, spb_r[s3]], writes=[S32_r])
                            vcopy(Sbf[(n + 1) % 3][:, :], S32[:, :], reads=[S32_r], writes=[Sbf_r[(n + 1) % 3]])

                    def S3(n):
                        c, hh, i, kb, hd, po, ksl, ktt, diag, mi = info(n)
                        s3 = n % 3
                        psL, psL_r = psum()
                        psLs[n] = (psL, psL_r)
                        mm(psL[:, :], kT[po, c, ksl], qT[po, c, :], True, False,
                           reads=[kT_r[c][ktt], qT_r[c]], writes=[psL_r])
                        mm(psL[:, :], negtri_b, spb[s3][:, :], False, i == 0,
                           reads=[cb_r, spb_r[s3]], writes=[psL_r])
                        if i > 0:
                            mm(psL[:, :], negones_b, Sbf[n % 3][:, :], False, True,
                               reads=[cb_r, Sbf_r[n % 3]], writes=[psL_r])

                    def S4(n):
                        c, hh, i, kb, hd, po, ksl, ktt, diag, mi = info(n)
                        e2 = n % 2
                        if hh == 0 and i == 0:
                            psO_cur[c] = psum(hold=True)
                        psO, psO_r = psO_cur[c]
                        psL, psL_r = psLs.pop(n)
                        act(wTt[e2][:, :], psL[:, :], AF.Exp, reads=[psL_r], writes=[wT_r[e2]])
                        if diag:
                            vtt(wTt[e2][:, :], wTt[e2][:, :], sbmask[mi], ALU.mult,
                                reads=[wT_r[e2], cb_r], writes=[wT_r[e2]])
                        mm(psO[po, :], vtok[:, kb, hd * 64:(hd + 1) * 64], wTt[e2][:, :], i == 0, kb == 0,
                           reads=[vtok_r[kb], wT_r[e2]], writes=[psO_r])
                        if hh == 1 and kb == 0:
                            act(ysb[:, c, :], psO[:, :], AF.Copy, reads=[psO_r], writes=[ysb_r[c]])
                            psum_release(psO)

                    NI = len(items)
                    S1(0)
                    for a in range(NI + 2):
                        if a + 1 < NI:
                            S1(a + 1)
                        if a < NI:
                            S2(a)
                        if 0 <= a - 1 < NI:
                            S3(a - 1)
                        if 0 <= a - 2 < NI:
                            S4(a - 2)
                K.barrier()
                mark('P4 l%d t%d' % (l, tt))
                ycat = [(zs[:, c, :], zs_r[c]) for c in range(4)] + [(ysb[:, c, :], ysb_r[c]) for c in range(2)] + \
                       [(ypl[:, c, :], ypl_r[c]) for c in range(2)]
                if debug and l == 0:
                    dby_v = dby_d.rearrange("(c p) t -> p c t", p=128)
                    for kc in range(8):
                        K.dma("pool", dby_v[:, kc, tsl], ycat[kc][0], reads=[ycat[kc][1]], writes=[dbg_res])
                for oc in range(8):
                    wa, wa_r = SA.get()
                    ps, ps_r = psum()
                    for kc in range(8):
                        mm(ps[:, :], wa[:, kc, :], ycat[kc][0], kc == 0, kc == 7,
                           reads=[wa_r, ycat[kc][1]], writes=[ps_r])
                    vtt(xT[:, oc, tsl], xT[:, oc, tsl], ps[:, :], ALU.add,
                        reads=[xT_r[oc][tt], ps_r], writes=[xT_r[oc][tt]])
                if debug and l == 0:
                    dbx_v = dbx_d.rearrange("(c p) t -> p c t", p=128)
                    for oc in range(8):
                        K.dma("sp", dbx_v[:, oc, tsl], xT[:, oc, tsl], reads=[xT_r[oc][tt]], writes=[dbg_res])
                mark('P5 l%d t%d' % (l, tt))
                with ExitStack() as ph:
                    tb = lambda name, shape, dt: ph.enter_context(nc.sbuf_tensor(uname(name), shape, dt))
                    sqb = tb("sqb3", [128, 8, TT], BF16); sqb_r = R(8)
                    lnv = tb("lnv3", [128, TT], F32); lnv_r = R()
                    rstd = tb("rstd3", [128, TT], F32); rstd_r = R()
                    aT = tb("aT", [128, NFC, TT], BF16); aT_r = R(NFC)
                    sg = [tb("sg%d" % i, [128, TT], F32) for i in range(2)]; sg_r = R(2)
                    rmsnorm_tile([xT[:, c, tsl] for c in range(8)], [xT_r[c][tt] for c in range(8)], 8,
                                 [pr[:, P_N2W + c:P_N2W + c + 1] for c in range(8)], pr_r,
                                 lambda c: hT[:, c, :], hT_r, D, (sqb, sqb_r, lnv, lnv_r, rstd, rstd_r))
                    for fc in range(NFC):
                        wg, wg_r = SA.get()
                        wu, wu_r = SA.get()
                        psG, psG_r = psum()
                        psU, psU_r = psum()
                        for kc in range(8):
                            mm(psG[:, :], wg[:, kc, :], hT[:, kc, :], kc == 0, kc == 7, reads=[wg_r, hT_r[kc]], writes=[psG_r])
                        for kc in range(8):
                            mm(psU[:, :], wu[:, kc, :], hT[:, kc, :], kc == 0, kc == 7, reads=[wu_r, hT_r[kc]], writes=[psU_r])
                        b2 = fc % 2
                        act(sg[b2][:, :], psG[:, :], AF.Silu, reads=[psG_r], writes=[sg_r[b2]])
                        vtt(aT[:, fc, :], sg[b2][:, :], psU[:, :], ALU.mult, reads=[sg_r[b2], psU_r], writes=[aT_r[fc]])
                    for oc in range(8):
                        wd, wd_r = SD.get()
                        ps, ps_r = psum()
                        for fc in range(NFC):
                            mm(ps[:, :], wd[:, fc, :], aT[:, fc, :], fc == 0, fc == NFC - 1,
                               reads=[wd_r, aT_r[fc]], writes=[ps_r])
                        vtt(xT[:, oc, tsl], xT[:, oc, tsl], ps[:, :], ALU.add,
                            reads=[xT_r[oc][tt], ps_r], writes=[xT_r[oc][tt]])
                    if l == depth - 1:
                        ost = [tb("ost%d" % i, [128, 4, TT], F32) for i in range(2)]; ost_r = [R(4), R(4)]
                        out_v = out_d.rearrange("(c p) t -> p c t", p=128)
                        rmsnorm_tile([xT[:, c, tsl] for c in range(8)], [xT_r[c][tt] for c in range(8)], 8,
                                     [fnw[:, c:c + 1] for c in range(8)], fnw_r,
                                     lambda c: ost[c // 4][:, c % 4, :], ost_r[0] + ost_r[1], D,
                                     (sqb, sqb_r, lnv, lnv_r, rstd, rstd_r))
                        for c in range(8):
                            tok = K.dma("sp", out_v[:, c, tsl], ost[c // 4][:, c % 4, :],
                                        reads=[ost_r[c // 4][c % 4]], writes=[out_res[c * NTT + tt]])
                K.barrier()
        mark("END")
        K.final_wait("sp", out_res + [dbg_res])
        nc._marks = marks
    return nc


def _tile_k(w):
    K_, N_ = w.shape
    return np.ascontiguousarray(w.reshape(K_ // 128, 128, N_).transpose(1, 0, 2)).reshape(128, -1)


def _consts():
    j = np.arange(128)[:, None]
    l = np.arange(128)[None, :]
    cf = np.zeros((128, CF_N), np.float32)
    cf[:, CF_IDENT:CF_IDENT + 128] = (j == l)
    cf[:, CF_ONES:CF_ONES + 128] = 1.0
    cf[:, CF_TRI:CF_TRI + 128] = (j <= l)
    cf[:, CF_RCFIX:CF_RCFIX + 16] = 1.0 / (np.arange(16)[None, :] + 1.0)
    cb = np.zeros((128, CB_N), np.float32)
    cb[:, CB_IDENT:CB_IDENT + 128] = (j == l)
    cb[:, CB_ONES:CB_ONES + 128] = 1.0
    cb[:, CB_NEGTRI:CB_NEGTRI + 128] = -1.0 * (j >= l)
    cb[:, CB_NEGONES:CB_NEGONES + 128] = -1.0
    t = np.arange(512)[None, :]
    for i in range(4):
        cb[:, CB_MASK + i * 512:CB_MASK + (i + 1) * 512] = ((128 * i + j) < t)
    return cf, cb


def prep_weights(inp, depth):
    f = lambda a: np.asarray(a, dtype=np.float32)
    w_in, w_out = f(inp["w_in"]), f(inp["w_out"])
    w_gate, w_up, w_down = f(inp["w_gate"]), f(inp["w_up"]), f(inp["w_down"])
    fm_cols = [c * 128 for c in range(4)] + [512 + c * 128 for c in range(8)] + [1544, 1672, 1800, 1928, 2312, 2440]
    w_fm = np.empty((depth, 18, 128, 1024), np.float32)
    w_tm = np.empty((depth, 128, 8 * 264), np.float32)
    w_o = np.empty((depth, 8, 128, 1024), np.float32)
    w_gu = np.empty((depth, NFC, 2, 128, 1024), np.float32)
    w_dn = np.empty((depth, 8, 128, NFC * 128), np.float32)
    par = np.zeros((depth, 128, NPAR), np.float32)
    pw = np.zeros((depth, 128, 256), np.float32)
    for l in range(depth):
        for i, c0 in enumerate(fm_cols):
            w_fm[l, i] = _tile_k(w_in[l][:, c0:c0 + 128])
        w_tm[l] = _tile_k(np.concatenate([w_in[l][:, 2056:2312], w_in[l][:, 1536:1544]], axis=1))
        for oc in range(8):
            w_o[l, oc] = _tile_k(w_out[l][:, oc * 128:(oc + 1) * 128])
            w_dn[l, oc] = _tile_k(w_down[l][:, oc * 128:(oc + 1) * 128])
        for fc in range(NFC):
            w_gu[l, fc, 0] = _tile_k(w_gate[l][:, fc * 128:(fc + 1) * 128])
            w_gu[l, fc, 1] = _tile_k(w_up[l][:, fc * 128:(fc + 1) * 128])
        cm = lambda v: np.asarray(v, np.float32).reshape(-1, 128).T
        par[l, :, P_N1W:P_N1W + 8] = cm(inp["norm1_w"][l])
        par[l, :, P_N2W:P_N2W + 8] = cm(inp["norm2_w"][l])
        cw = f(inp["conv_w"][l])
        for i in range(4):
            par[l, :, P_CW + i:P_CW + 32:4] = cm(cw[i])
        par[l, :, P_CB:P_CB + 8] = cm(inp["conv_b"][l])
        par[l, :, P_SNW:P_SNW + 4] = cm(inp["ssd_norm_w"][l])
        par[l, :, P_PB:P_PB + 2] = cm(f(inp["pool_b"][l]).reshape(-1))
        par[l, :, P_PS:P_PS + 2] = cm(inp["pool_scale"][l])
        par[l, :, P_DTB:P_DTB + 8] = f(inp["dt_bias"][l])[None, :]
        par[l, :, P_ALOG:P_ALOG + 8] = f(inp["a_log"][l])[None, :]
        par[l, :, P_DSK:P_DSK + 8] = f(inp["d_skip"][l])[None, :]
        pwl = f(inp["pool_w"][l])
        for i in range(2):
            for hh in range(2):
                pw[l, hh * 64:(hh + 1) * 64, i * 128 + hh * 64:i * 128 + (hh + 1) * 64] = pwl[2 * i + hh]
    cf, cb = _consts()
    fnw = np.ascontiguousarray(f(inp["final_norm_w"]).reshape(8, 128).T)
    return dict(w_fm=w_fm, w_tm=w_tm, w_o=w_o, w_gu=w_gu, w_dn=w_dn, par=par, pw=pw, fnw=fnw, cf=cf, cb=cb)


_PROG = {}


def run(inputs, depth=DEPTH, cores=NCORES, debug=False):
    if (depth, debug) not in _PROG:
        _PROG[(depth, debug)] = build_program(depth, debug)
    nc = _PROG[(depth, debug)]
    wts = prep_weights(inputs, depth)
    x = np.asarray(inputs["x"], dtype=np.float32)
    in_maps = []
    for b in range(cores):
        m = dict(wts)
        m["xT"] = np.ascontiguousarray(x[b].T)
        in_maps.append(m)
    res = run_bass_kernel_spmd(nc, in_maps, core_ids=list(range(cores)))
    out = np.stack([np.ascontiguousarray(r["outT"].T) for r in res.results], axis=0)
    if debug:
        return out.astype(np.float32), [dict(r) for r in res.results]
    return out.astype(np.float32)


def kernel(**inputs):
    return run(inputs, DEPTH, NCORES)
```

```python
import numpy as np
from contextlib import ExitStack
import concourse.bass as bass
import concourse.mybir as mybir
from concourse.bass_utils import run_bass_kernel_spmd

F32 = mybir.dt.float32
BF16 = mybir.dt.bfloat16
AF = mybir.ActivationFunctionType
ALU = mybir.AluOpType

D = 1024
T = 2048
DEPTH = 4
NCORES = 8
DFF = 2816
NFC = DFF // 128
TT = 512
NTT = T // TT
NBT = TT // 128
NB = T // 128
EPS = 1e-6
NPAR = 88

P_N1W, P_N2W, P_CW, P_CB, P_SNW, P_PB, P_PS, P_DTB, P_ALOG, P_DSK = 0, 8, 16, 48, 56, 60, 62, 64, 72, 80
CF_IDENT, CF_ONES, CF_TRI, CF_RCFIX, CF_N = 0, 128, 256, 384, 400
CB_IDENT, CB_ONES, CB_NEGTRI, CB_NEGONES, CB_MASK, CB_N = 0, 128, 256, 384, 512, 512 + 4 * 512


class Res:
    __slots__ = ("w", "r")

    def __init__(self):
        self.w = None
        self.r = {}


class Eng:
    def __init__(self, name, h, sem):
        self.name, self.h, self.sem = name, h, sem
        self.count = 0
        self.waited = {}


class KB:
    def __init__(self, nc, stack, n_dma_sems=40):
        self.nc = nc
        self.eng = {}
        for name, h in (("pe", nc.tensor), ("act", nc.scalar), ("dve", nc.vector), ("pool", nc.gpsimd), ("sp", nc.sync)):
            sem = stack.enter_context(nc.semaphore("sem_" + name))
            self.eng[name] = Eng(name, h, sem)
        self.dsem = [stack.enter_context(nc.semaphore("dsem%d" % i)) for i in range(n_dma_sems)]
        self.dval = [0] * n_dma_sems
        self.dnext = 0
        self.semkey = {}

    def _key(self, sem):
        return id(sem)

    def _deps(self, e, reads, writes):
        need = {}

        def add(tok, raw):
            sem, val = tok
            if sem is e.sem and (e.name == "pe" or not raw):
                return
            k = id(sem)
            if k not in need or need[k][1] < val:
                need[k] = (sem, val)

        for r in reads:
            if r.w is not None:
                add(r.w, True)
        for w in writes:
            if w.w is not None:
                add(w.w, False)
            for k, tok in w.r.items():
                add(tok, False)
        for k, (sem, val) in need.items():
            if e.waited.get(k, 0) < val:
                e.h.wait_ge(sem, val)
                e.waited[k] = val

    def _mark(self, tok, reads, writes):
        k = id(tok[0])
        for r in reads:
            r.r[k] = tok
        for w in writes:
            w.w = tok
            w.r = {}

    def op(self, en, fn, reads=(), writes=()):
        e = self.eng[en]
        self._deps(e, reads, writes)
        ins = fn(e.h)
        e.count += 1
        ins.then_inc(e.sem, 1)
        tok = (e.sem, e.count)
        self._mark(tok, reads, writes)
        return tok

    def dma(self, qn, out, in_, reads=(), writes=()):
        q = self.eng[qn]
        slot = self.dnext
        self.dnext = (slot + 1) % len(self.dsem)
        sem = self.dsem[slot]
        self._deps(q, reads, writes)
        if self.dval[slot] > 0 and q.waited.get(id(sem), 0) < self.dval[slot]:
            q.h.wait_ge(sem, self.dval[slot])
            q.waited[id(sem)] = self.dval[slot]
        ins = q.h.dma_start(out=out, in_=in_)
        self.dval[slot] += 16
        ins.then_inc(sem, 16)
        tok = (sem, self.dval[slot])
        self._mark(tok, reads, writes)
        return tok

    def barrier(self):
        toks = [(e.sem, e.count) for e in self.eng.values() if e.count > 0]
        for e in self.eng.values():
            for sem, val in toks:
                if sem is e.sem:
                    continue
                if e.waited.get(id(sem), 0) < val:
                    e.h.wait_ge(sem, val)
                    e.waited[id(sem)] = val

    def final_wait(self, en, reslist):
        e = self.eng[en]
        self._deps(e, reslist, [])


def build_program(depth=DEPTH, debug=False):
    nc = bass.Bass("TRN2", target_bir_lowering=False)
    dt_ = lambda n, s: nc.dram_tensor(n, s, F32, kind="ExternalInput").ap()
    xT_d = dt_("xT", [D, T])
    wfm_d = dt_("w_fm", [depth, 18, 128, 8 * 128])
    wtm_d = dt_("w_tm", [depth, 128, 8 * 264])
    wout_d = dt_("w_o", [depth, 8, 128, 8 * 128])
    wgu_d = dt_("w_gu", [depth, NFC, 2, 128, 8 * 128])
    wdn_d = dt_("w_dn", [depth, 8, 128, NFC * 128])
    par_d = dt_("par", [depth, 128, NPAR])
    pw_d = dt_("pw", [depth, 128, 2 * 128])
    fnw_d = dt_("fnw", [128, 8])
    cf_d = dt_("cf", [128, CF_N])
    cb_d = dt_("cb", [128, CB_N])
    out_d = nc.dram_tensor("outT", [D, T], F32, kind="ExternalOutput").ap()
    if debug:
        dby_d = nc.dram_tensor("dbg_y", [D, T], F32, kind="ExternalOutput").ap()
        dbx_d = nc.dram_tensor("dbg_x", [D, T], F32, kind="ExternalOutput").ap()

    with ExitStack() as st:
        K = KB(nc, st)
        uid = [0]

        def uname(name):
            uid[0] += 1
            return "s_%s_%d" % (name, uid[0])

        sb = lambda name, shape, dt: st.enter_context(nc.sbuf_tensor(uname(name), shape, dt))

        def R(n=None):
            return Res() if n is None else [Res() for _ in range(n)]

        xT = sb("xT", [128, 8, T], F32)
        xT_r = [[Res() for _ in range(NTT)] for _ in range(8)]
        kT = sb("kT", [128, 2, T], BF16)
        kT_r = [[Res() for _ in range(NTT)] for _ in range(2)]
        vtok = sb("vtok", [128, NB, 256], BF16)
        vtok_r = R(NB)
        hT = sb("hT", [128, 8, TT], BF16)
        hT_r = R(8)
        zs = sb("zs", [128, 4, TT], BF16)
        zs_r = R(4)
        ysb = sb("ysb", [128, 2, TT], BF16)
        ysb_r = R(2)
        ypl = sb("ypl", [128, 2, TT], BF16)
        ypl_r = R(2)
        xstok = sb("xstok", [128, NBT, 512], BF16)
        xstok_r = R(NBT)
        BTt = sb("BT", [128, 2, TT], BF16)
        BT_r = R(2)
        Btok = sb("Btok", [128, NBT, 256], BF16)
        Btok_r = R(NBT)
        CTt = sb("CT", [128, 2, TT], BF16)
        CT_r = R(2)
        qT = sb("qT", [128, 2, TT], BF16)
        qT_r = R(2)
        cf = sb("cf", [128, CF_N], F32)
        cf_r = R()
        cb = sb("cb", [128, CB_N], BF16)
        cb_r = R()
        par = [sb("par%d" % i, [128, NPAR], F32) for i in range(2)]
        par_r = R(2)
        fnw = sb("fnw", [128, 8], F32)
        fnw_r = R()
        DI = sb("DI", [128, 8, 128], BF16)
        DI_r = R()
        pwbd = sb("pwbd", [128, 2, 128], BF16)
        pwbd_r = R()
        arep = sb("arep", [128, 8], F32)
        arep_r = R()
        prev32 = sb("prev32", [128, 8, 64], F32)
        prev32_r = R()
        prevbf = sb("prevbf", [128, 8, 64], BF16)
        prevbf_r = R()
        chalo = sb("chalo", [128, 8, 4], F32)
        chalo_r = R(8)
        pst = sb("pst", [128, 2, 16 + TT], F32)
        pst_r = R(2)
        smalls = {}
        for nm in ("dtr", "ee", "dtt", "dA", "acum", "tot", "cd", "dd", "dte", "xsc"):
            smalls[nm] = (sb("sm_" + nm, [128, NBT, 8], F32), Res())
        NA, NTM, ND = 6, 1, 2
        wA = [sb("wA%d" % i, [128, 8, 128], BF16) for i in range(NA)]
        wA_r = R(NA)
        wTm = [sb("wT%d" % i, [128, 8, 264], BF16) for i in range(NTM)]
        wTm_r = R(NTM)
        wD = [sb("wD%d" % i, [128, NFC, 128], BF16) for i in range(ND)]
        wD_r = R(ND)
        psb = [st.enter_context(nc.psum_tensor("ps%d" % i, [128, 512], F32)) for i in range(8)]
        psb_r = R(8)
        psn = [0]

        held = set()

        def psum(hold=False):
            i = psn[0]
            while i in held:
                i = (i + 1) % 8
            psn[0] = (i + 1) % 8
            if hold:
                held.add(i)
            return psb[i], psb_r[i]

        def psum_release(ps):
            for i in range(8):
                if psb[i] is ps:
                    held.discard(i)

        ident_f = cf[:, CF_IDENT:CF_IDENT + 128]
        ones_f = cf[:, CF_ONES:CF_ONES + 128]
        tri_f = cf[:, CF_TRI:CF_TRI + 128]
        rcfix = cf[:, CF_RCFIX:CF_RCFIX + 16]
        ident_b = cb[:, CB_IDENT:CB_IDENT + 128]
        ones_b = cb[:, CB_ONES:CB_ONES + 128]
        negtri_b = cb[:, CB_NEGTRI:CB_NEGTRI + 128]
        negones_b = cb[:, CB_NEGONES:CB_NEGONES + 128]
        sbmask = [cb[:, CB_MASK + i * 512:CB_MASK + (i + 1) * 512] for i in range(4)]

        K.dma("sp", cf[:], cf_d[:], writes=[cf_r])
        K.dma("pool", cb[:], cb_d[:], writes=[cb_r])
        K.dma("sp", fnw[:], fnw_d[:], writes=[fnw_r])
        xT_v = xT_d.rearrange("(c p) t -> p c t", p=128)
        for c in range(8):
            K.dma("sp", xT[:, c, :], xT_v[:, c, :], writes=xT_r[c])

        loadsA, loadsT, loadsD = [], [], []
        for l in range(depth):
            for tt in range(NTT):
                loadsT.append(wtm_d[l])
                for oc in range(18):
                    loadsA.append(wfm_d[l, oc])
                for oc in range(8):
                    loadsA.append(wout_d[l, oc])
                for fc in range(NFC):
                    loadsA.append(wgu_d[l, fc, 0])
                    loadsA.append(wgu_d[l, fc, 1])
                for oc in range(8):
                    loadsD.append(wdn_d[l, oc])

        class Stream:
            def __init__(self, loads, tiles, res, shp):
                self.loads, self.tiles, self.res, self.shp = loads, tiles, res, shp
                self.issued = 0
                self.used = 0

            def get(self):
                i = self.used
                n = len(self.tiles)
                while self.issued < len(self.loads) and self.issued <= i + n - self.shp:
                    j = self.issued
                    tl = self.tiles[j % n]
                    K.dma("pool", tl[:].rearrange("p a b -> p (a b)"), self.loads[j], writes=[self.res[j % n]])
                    self.issued += 1
                self.used += 1
                return self.tiles[i % n], self.res[i % n]

        SA = Stream(loadsA, wA, wA_r, 2)
        STm = Stream(loadsT, wTm, wTm_r, 1)
        SD = Stream(loadsD, wD, wD_r, 1)

        def mm(out, lhsT, rhs, start, stop, reads, writes):
            return K.op("pe", lambda h: h.matmul(out, lhsT, rhs, start=start, stop=stop), reads=reads, writes=writes)

        def act(out, in_, func, reads, writes, bias=None, scale=None):
            kw = {}
            if bias is not None:
                kw["bias"] = bias
            if scale is not None:
                kw["scale"] = scale
            return K.op("act", lambda h: h.activation(out=out, in_=in_, func=func, **kw), reads=reads, writes=writes)

        def vtt(out, in0, in1, op, reads, writes, en="dve"):
            return K.op(en, lambda h: h.tensor_tensor(out, in0, in1, op), reads=reads, writes=writes)

        def vts(out, in0, s1, s2, op0, op1, reads, writes, en="dve"):
            if s2 is None:
                return K.op(en, lambda h: h.tensor_scalar(out, in0, s1, None, op0), reads=reads, writes=writes)
            return K.op(en, lambda h: h.tensor_scalar(out, in0, s1, s2, op0, op1), reads=reads, writes=writes)

        def vstt(out, in0, sc, in1, op0, op1, reads, writes, en="dve"):
            return K.op(en, lambda h: h.scalar_tensor_tensor(out, in0, sc, in1, op0, op1), reads=reads, writes=writes)

        def vcopy(out, in_, reads, writes, en="dve"):
            return K.op(en, lambda h: h.tensor_copy(out, in_), reads=reads, writes=writes)

        def vmemset(ap, val, writes, en="dve"):
            return K.op(en, lambda h: h.memset(ap, val), writes=writes)

        def bc_mid(ap2, n):
            return ap2.unsqueeze(1).broadcast_to([ap2.shape[0], n, ap2.shape[1]])

        def bc_last(ap2, n):
            return ap2.unsqueeze(2).broadcast_to([ap2.shape[0], ap2.shape[1], n])

        dbg_outs = {}
        marks = []

        def mark(name):
            marks.append((name, {n: e.count for n, e in K.eng.items()}))
        out_res = [Res() for _ in range(8 * NTT)]
        dbg_res = Res()

        def rmsnorm_tile(src_chunks, src_res, nchunk, wcols, wres, dst_fn, dst_res, nfeat, tmp):
            sqb, sqb_r, lnv, lnv_r, rstd, rstd_r = tmp
            for c in range(nchunk):
                act(sqb[:, c, :], src_chunks[c], AF.Square, reads=[src_res[c]], writes=[sqb_r[c]])
            ps, ps_r = psum()
            for c in range(nchunk):
                mm(ps[:, :], ones_b, sqb[:, c, :], c == 0, c == nchunk - 1, reads=[cb_r, sqb_r[c]], writes=[ps_r])
            act(lnv[:, :], ps[:, :], AF.Ln, reads=[ps_r], writes=[lnv_r], bias=EPS, scale=1.0 / nfeat)
            act(rstd[:, :], lnv[:, :], AF.Exp, reads=[lnv_r], writes=[rstd_r], scale=-0.5)
            for c in range(nchunk):
                vstt(dst_fn(c), src_chunks[c], wcols[c], rstd[:, :], ALU.mult, ALU.mult,
                     reads=[src_res[c], wres, rstd_r], writes=[dst_res[c]])

        for l in range(depth):
            pr = par[l % 2]
            pr_r = par_r[l % 2]
            K.dma("sp", pr[:], par_d[l], writes=[pr_r])
            K.dma("pool", pwbd[:].rearrange("p a b -> p (a b)"), pw_d[l], writes=[pwbd_r])
            vmemset(prev32[:], 0.0, [prev32_r])
            vmemset(prevbf[:], 0.0, [prevbf_r])
            vmemset(chalo[:], 0.0, chalo_r)
            vmemset(pst[:], 0.0, pst_r)
            act(arep[:, :], pr[:, P_ALOG:P_ALOG + 8], AF.Exp, reads=[pr_r], writes=[arep_r])
            vts(arep[:, :], arep[:, :], -1.0, None, ALU.mult, None, reads=[arep_r], writes=[arep_r])
            for h in range(8):
                vts(DI[:, h, :], ident_f, pr[:, P_DSK + h:P_DSK + h + 1], None, ALU.mult, None,
                    reads=[cf_r, pr_r], writes=[DI_r])

            for tt in range(NTT):
                tsl = slice(tt * TT, (tt + 1) * TT)
                mark('P1 l%d t%d' % (l, tt))
                with ExitStack() as ph:
                    tb = lambda name, shape, dt: ph.enter_context(nc.sbuf_tensor(uname(name), shape, dt))
                    sqb = tb("sqb", [128, 8, TT], BF16); sqb_r = R(8)
                    lnv = tb("lnv", [128, TT], F32); lnv_r = R()
                    rstd = tb("rstd", [128, TT], F32); rstd_r = R()
                    cst = [tb("cst%d" % i, [128, 4 + TT], F32) for i in range(2)]; cst_r = R(2)
                    ctmp = [tb("ctmp%d" % i, [128, TT], F32) for i in range(2)]; ctmp_r = R(2)
                    cact = [tb("cact%d" % i, [128, TT], BF16) for i in range(2)]; cact_r = R(2)
                    s2 = tb("s2", [128, 16 + TT], F32); s2_r = R()
                    s4 = tb("s4", [128, 16 + TT], F32); s4_r = R()
                    s8 = tb("s8", [128, 16 + TT], F32); s8_r = R()
                    s16 = tb("s16", [128, 16 + TT], F32); s16_r = R()
                    pfix = tb("pfix", [128, 16], F32); pfix_r = R()
                    pooled = [tb("pooled%d" % i, [128, TT], BF16) for i in range(2)]; pooled_r = R(2)
                    dtr, dtr_r = smalls["dtr"]

                    rmsnorm_tile([xT[:, c, tsl] for c in range(8)], [xT_r[c][tt] for c in range(8)], 8,
                                 [pr[:, P_N1W + c:P_N1W + c + 1] for c in range(8)], pr_r,
                                 lambda c: hT[:, c, :], hT_r, D, (sqb, sqb_r, lnv, lnv_r, rstd, rstd_r))

                    wt, wt_r = STm.get()
                    for blk in range(NBT):
                        ps, ps_r = psum()
                        bsl = slice(blk * 128, (blk + 1) * 128)
                        for kc in range(8):
                            mm(ps[:, 0:264], hT[:, kc, bsl], wt[:, kc, :], kc == 0, kc == 7,
                               reads=[hT_r[kc], wt_r], writes=[ps_r])
                        gb = tt * NBT + blk
                        act(vtok[:, gb, :], ps[:, 0:256], AF.Copy, reads=[ps_r], writes=[vtok_r[gb]])
                        vtt(dtr[:, blk, :], ps[:, 256:264], pr[:, P_DTB:P_DTB + 8], ALU.add,
                            reads=[ps_r, pr_r], writes=[dtr_r])

                    pendingY = []
                    for oc in range(18):
                        wa, wa_r = SA.get()
                        ps, ps_r = psum()
                        for kc in range(8):
                            mm(ps[:, :], wa[:, kc, :], hT[:, kc, :], kc == 0, kc == 7,
                               reads=[wa_r, hT_r[kc]], writes=[ps_r])
                        while pendingY:
                            pendingY.pop(0)()
                        if oc < 4:
                            act(zs[:, oc, :], ps[:, :], AF.Silu, reads=[ps_r], writes=[zs_r[oc]])
                        elif oc < 12:
                            j = oc - 4
                            sl_ = j % 2
                            cs, cs_r = cst[sl_], cst_r[sl_]
                            ct, ct_r = ctmp[sl_], ctmp_r[sl_]
                            ca, ca_r = cact[sl_], cact_r[sl_]
                            vcopy(cs[:, 0:3], chalo[:, j, 0:3], reads=[chalo_r[j]], writes=[cs_r])
                            act(cs[:, 3:3 + TT], ps[:, :], AF.Copy, reads=[ps_r], writes=[cs_r])
                            vcopy(chalo[:, j, 0:3], cs[:, TT:TT + 3], reads=[cs_r], writes=[chalo_r[j]])
                            cw = lambda i: pr[:, P_CW + j * 4 + i:P_CW + j * 4 + i + 1]
                            vts(ct[:, :], cs[:, 0:TT], cw(0), pr[:, P_CB + j:P_CB + j + 1], ALU.mult, ALU.add,
                                reads=[cs_r, pr_r], writes=[ct_r])
                            for i in range(1, 4):
                                vstt(ct[:, :], cs[:, i:i + TT], cw(i), ct[:, :], ALU.mult, ALU.add,
                                     reads=[cs_r, pr_r, ct_r], writes=[ct_r])
                            if j < 4:
                                act(ca[:, :], ct[:, :], AF.Silu, reads=[ct_r], writes=[ca_r])
                                src, src_r = ca, ca_r
                            elif j < 6:
                                g = j - 4
                                act(BTt[:, g, :], ct[:, :], AF.Silu, reads=[ct_r], writes=[BT_r[g]])
                                src, src_r = None, BT_r[g]
                            else:
                                g = j - 6
                                act(CTt[:, g, :], ct[:, :], AF.Silu, reads=[ct_r], writes=[CT_r[g]])
                            def doY(j=j, ca=ca, src_r=src_r):
                                pt, pt_r = psum()
                                ptb = pt[:, :].bitcast(BF16)
                                for blk in range(NBT):
                                    bsl = slice(blk * 128, (blk + 1) * 128)
                                    srcap = ca[:, bsl] if j < 4 else BTt[:, j - 4, bsl]
                                    K.op("pe", lambda h, o=ptb[:, blk * 128:(blk + 1) * 128], s=srcap: h.transpose(o, s, ident_b),
                                         reads=[src_r, cb_r], writes=[pt_r])
                                if j < 4:
                                    vcopy(xstok[:, :, j * 128:(j + 1) * 128],
                                          ptb[:, 0:NBT * 128].rearrange("p (b c) -> p b c", b=NBT),
                                          reads=[pt_r], writes=xstok_r)
                                else:
                                    g = j - 4
                                    vcopy(Btok[:, :, g * 128:(g + 1) * 128],
                                          ptb[:, 0:NBT * 128].rearrange("p (b c) -> p b c", b=NBT),
                                          reads=[pt_r], writes=Btok_r)
                            if j < 6:
                                pendingY.append(doY)
                        elif oc < 14:
                            i = oc - 12
                            K.op('act', lambda h, o=qT[:, i, :], s=ps[:, :]: h.mul(o, s, 0.125), reads=[ps_r], writes=[qT_r[i]])
                        elif oc < 16:
                            i = oc - 14
                            act(kT[:, i, tsl], ps[:, :], AF.Copy, reads=[ps_r], writes=[kT_r[i][tt]])
                        else:
                            i = oc - 16
                            A = pst[:, i, :]
                            A_r = pst_r[i]
                            W = 16 + TT
                            act(A[:, 16:W], ps[:, :], AF.Copy, reads=[ps_r], writes=[A_r])
                            vtt(s2[:, 1:W], A[:, 1:W], A[:, 0:W - 1], ALU.add, reads=[A_r], writes=[s2_r])
                            vtt(s4[:, 3:W], s2[:, 3:W], s2[:, 1:W - 2], ALU.add, reads=[s2_r], writes=[s4_r])
                            if i == 0:
                                lev = [(s2, s2_r, 2), (s4, s4_r, 4)]
                            else:
                                vtt(s8[:, 7:W], s4[:, 7:W], s4[:, 3:W - 4], ALU.add, reads=[s4_r], writes=[s8_r])
                                vtt(s16[:, 15:W], s8[:, 15:W], s8[:, 7:W - 8], ALU.add, reads=[s8_r], writes=[s16_r])
                                lev = [(s8, s8_r, 8), (s16, s16_r, 16)]
                            pl, pl_r = pooled[i], pooled_r[i]
                            for half, (sv, sv_r, win) in enumerate(lev):
                                psl = slice(half * 64, (half + 1) * 64)
                                vstt(pl[psl, :], sv[psl, 16:W], 1.0 / win, A[psl, 16:W], ALU.mult, ALU.subtract,
                                     reads=[sv_r, A_r], writes=[pl_r])
                                if tt == 0 and win > 1:
                                    nfx = win - 1
                                    vtt(pfix[psl, 0:nfx], sv[psl, 16:16 + nfx], rcfix[psl, 0:nfx], ALU.mult,
                                        reads=[sv_r, cf_r], writes=[pfix_r])
                                    vtt(pl[psl, 0:nfx], pfix[psl, 0:nfx], A[psl, 16:16 + nfx], ALU.subtract,
                                        reads=[pfix_r, A_r], writes=[pl_r])
                            vcopy(A[:, 0:16], A[:, TT:TT + 16], reads=[A_r], writes=[A_r])
                            ps2, ps2_r = psum()
                            mm(ps2[:, :], pwbd[:, i, :], pl[:, :], True, True, reads=[pwbd_r, pl_r], writes=[ps2_r])
                            vts(ypl[:, i, :], ps2[:, :], pr[:, P_PB + i:P_PB + i + 1], pr[:, P_PS + i:P_PS + i + 1],
                                ALU.add, ALU.mult, reads=[ps2_r, pr_r], writes=[ypl_r[i]])


                    while pendingY:
                        pendingY.pop(0)()
                    ee, ee_r = smalls["ee"]
                    dtt, dtt_r = smalls["dtt"]
                    dA, dA_r = smalls["dA"]
                    acum, acum_r = smalls["acum"]
                    tot, tot_r = smalls["tot"]
                    cd, cd_r = smalls["cd"]
                    dd, dd_r = smalls["dd"]
                    dte, dte_r = smalls["dte"]
                    xsc, xsc_r = smalls["xsc"]
                    act(ee[:], dtr[:], AF.Exp, reads=[dtr_r], writes=[ee_r])
                    act(dtt[:], ee[:], AF.Ln, reads=[ee_r], writes=[dtt_r], bias=1.0)
                    vtt(dA[:], dtt[:], bc_mid(arep[:, :], NBT), ALU.mult, reads=[dtt_r, arep_r], writes=[dA_r])
                    ps, ps_r = psum()
                    dA2 = dA[:].rearrange("p b h -> p (b h)")
                    mm(ps[:, 0:NBT * 8], tri_f, dA2, True, True, reads=[cf_r, dA_r], writes=[ps_r])
                    mm(ps[:, 64:64 + NBT * 8], ones_f, dA2, True, True, reads=[cf_r, dA_r], writes=[ps_r])
                    vcopy(acum[:].rearrange("p b h -> p (b h)"), ps[:, 0:NBT * 8], reads=[ps_r], writes=[acum_r])
                    vcopy(tot[:].rearrange("p b h -> p (b h)"), ps[:, 64:64 + NBT * 8], reads=[ps_r], writes=[tot_r])
                    act(cd[:], tot[:], AF.Exp, reads=[tot_r], writes=[cd_r])
                    vtt(dd[:], tot[:], acum[:], ALU.subtract, reads=[tot_r, acum_r], writes=[dd_r])
                    act(dte[:], dd[:], AF.Exp, reads=[dd_r], writes=[dte_r])
                    vtt(xsc[:], dte[:], dtt[:], ALU.mult, reads=[dte_r, dtt_r], writes=[xsc_r])
                K.barrier()
                mark('P2 l%d t%d' % (l, tt))
                with ExitStack() as ph:
                    tb = lambda name, shape, dt: ph.enter_context(nc.sbuf_tensor(uname(name), shape, dt))
                    Rm = tb("Rm", [128, 8, 128], F32); Rm_r = R()
                    seg = [tb("seg%d" % i, [128, 4, 128], F32) for i in range(2)]; seg_r = R(2)
                    eex = [tb("eex%d" % i, [128, 4, 128], F32) for i in range(2)]; eex_r = R(2)
                    Mm = [tb("Mm%d" % i, [128, 4, 128], BF16) for i in range(2)]; Mm_r = R(2)
                    Eh = [tb("Eh%d" % i, [128, 4, 128], BF16) for i in range(2)]; Eh_r = R(2)
                    Cs = [tb("Cs%d" % i, [128, 4, 128], BF16) for i in range(2)]; Cs_r = R(2)
                    cbm = tb("cbm", [128, 2, 128], F32); cbm_r = R()
                    Xs = tb("Xs", [128, 8, 64], BF16); Xs_r = R()
                    Xd = tb("Xd", [128, 8, 64], BF16); Xd_r = R()
                    ptmp = tb("ptmp", [128, 8, 64], F32); ptmp_r = R()
                    yg = tb("yg", [128, 4, TT], F32); yg_r = R(4)
                    sqb = tb("sqb2", [128, 4, TT], BF16); sqb_r = R(4)
                    lnv = tb("lnv2", [128, TT], F32); lnv_r = R()
                    rstd = tb("rstd2", [128, TT], F32); rstd_r = R()
                    for blk in range(NBT):
                        bsl = slice(blk * 128, (blk + 1) * 128)
                        vtt(Rm[:], bc_mid(tri_f, 8), bc_last(dA[:, blk, :], 128), ALU.mult,
                            reads=[cf_r, dA_r], writes=[Rm_r])
                        psA, psA_r = psum()
                        psB, psB_r = psum()
                        mm(psA[:, :], ones_f, Rm[:, 0:4, :].rearrange("p a b -> p (a b)"), True, True,
                           reads=[cf_r, Rm_r], writes=[psA_r])
                        mm(psB[:, :], ones_f, Rm[:, 4:8, :].rearrange("p a b -> p (a b)"), True, True,
                           reads=[cf_r, Rm_r], writes=[psB_r])
                        psC, psC_r = psum()
                        for g in range(2):
                            mm(psC[:, g * 128:(g + 1) * 128], BTt[:, g, bsl], CTt[:, g, bsl], True, True,
                               reads=[BT_r[g], CT_r[g]], writes=[psC_r])
                        vtt(cbm[:], psC[:, 0:256].rearrange("p (g l) -> p g l", g=2), bc_mid(tri_f, 2), ALU.mult,
                            reads=[psC_r, cf_r], writes=[cbm_r])
                        for g in range(2):
                            pbc, pbc_r = (psA, psA_r) if g == 0 else (psB, psB_r)
                            pbc3 = pbc[:, :].rearrange("p (k l) -> p k l", k=4)
                            vtt(seg[g][:], pbc3, bc_last(acum[:, blk, 4 * g:4 * g + 4], 128), ALU.subtract,
                                reads=[pbc_r, acum_r], writes=[seg_r[g]])
                            act(eex[g][:], seg[g][:], AF.Exp, reads=[seg_r[g]], writes=[eex_r[g]])
                            vstt(Mm[g][:], eex[g][:], 1.0, bc_mid(cbm[:, g, :], 4), ALU.min, ALU.mult,
                                 reads=[eex_r[g], cbm_r], writes=[Mm_r[g]])
                            act(Eh[g][:], pbc3, AF.Exp, reads=[pbc_r], writes=[Eh_r[g]])
                            vtt(Cs[g][:], Eh[g][:], bc_mid(CTt[:, g, bsl], 4), ALU.mult,
                                reads=[Eh_r[g], CT_r[g]], writes=[Cs_r[g]])
                        xs3 = xstok[:, blk, :].rearrange("p (h d) -> p h d", h=8)
                        vtt(Xs[:], xs3, bc_last(dtt[:, blk, :], 64), ALU.mult,
                            reads=[xstok_r[blk], dtt_r], writes=[Xs_r])
                        vtt(Xd[:], xs3, bc_last(xsc[:, blk, :], 64), ALU.mult,
                            reads=[xstok_r[blk], xsc_r], writes=[Xd_r])
                        psY, psY_r = psum()
                        for h in range(8):
                            g, k = h // 4, h % 4
                            o = psY[(h % 2) * 64:(h % 2 + 1) * 64, (h // 2) * 128:(h // 2 + 1) * 128]
                            mm(o, Xs[:, h, :], Mm[g][:, k, :], True, False, reads=[Xs_r, Mm_r[g]], writes=[psY_r])
                            mm(o, xstok[:, blk, h * 64:(h + 1) * 64], DI[:, h, :], False, False,
                               reads=[xstok_r[blk], DI_r], writes=[psY_r])
                            mm(o, prevbf[:, h, :], Cs[g][:, k, :], False, True, reads=[prevbf_r, Cs_r[g]], writes=[psY_r])
                        vtt(yg[:, :, bsl], psY[:, :].rearrange("p (c l) -> p c l", c=4), zs[:, :, bsl], ALU.mult,
                            reads=[psY_r] + zs_r, writes=yg_r)
                        psS, psS_r = psum()
                        for g in range(2):
                            mm(psS[:, g * 256:(g + 1) * 256], Btok[:, blk, g * 128:(g + 1) * 128],
                               Xd[:, 4 * g:4 * g + 4, :].rearrange("p a b -> p (a b)"), True, True,
                               reads=[Btok_r[blk], Xd_r], writes=[psS_r])
                        vtt(ptmp[:], prev32[:], bc_last(cd[:, blk, :], 64), ALU.mult,
                            reads=[prev32_r, cd_r], writes=[ptmp_r])
                        vtt(prev32[:], ptmp[:], psS[:, :].rearrange("p (h d) -> p h d", h=8), ALU.add,
                            reads=[ptmp_r, psS_r], writes=[prev32_r])
                        act(prevbf[:], prev32[:], AF.Copy, reads=[prev32_r], writes=[prevbf_r])
                    rmsnorm_tile([yg[:, c, :] for c in range(4)], yg_r, 4,
                                 [pr[:, P_SNW + c:P_SNW + c + 1] for c in range(4)], pr_r,
                                 lambda c: zs[:, c, :], zs_r, 512, (sqb, sqb_r, lnv, lnv_r, rstd, rstd_r))
                K.barrier()
                mark('P3 l%d t%d' % (l, tt))
                with ExitStack() as ph:
                    tb = lambda name, shape, dt: ph.enter_context(nc.sbuf_tensor(uname(name), shape, dt))
                    ez = [tb("ez%d" % i, [128, TT], F32) for i in range(2)]; ez_r = R(2)
                    spb = [tb("spb%d" % i, [128, TT], BF16) for i in range(3)]; spb_r = R(3)
                    wTt = [tb("wTt%d" % i, [128, TT], BF16) for i in range(2)]; wT_r = R(2)
                    S32 = tb("S32", [128, TT], F32); S32_r = R()
                    Sbf = [tb("Sbf%d" % i, [128, TT], BF16) for i in range(3)]; Sbf_r = R(3)
                    nkb = NBT * (tt + 1)
                    items = []
                    for c in range(2):
                        for hh in range(2):
                            for i, kb in enumerate(range(nkb - 1, -1, -1)):
                                items.append((c, hh, i, kb))
                    psO_cur = {}

                    psZs = {}
                    psLs = {}

                    def info(n):
                        c, hh, i, kb = items[n]
                        return (c, hh, i, kb, 2 * c + hh, slice(hh * 64, (hh + 1) * 64), slice(kb * 128, (kb + 1) * 128),
                                kb // NBT, kb >= NBT * tt, kb - NBT * tt)

                    def S1(n):
                        c, hh, i, kb, hd, po, ksl, ktt, diag, mi = info(n)
                        psZ, psZ_r = psum()
                        psZs[n] = (psZ, psZ_r)
                        mm(psZ[:, :], kT[po, c, ksl], qT[po, c, :], True, True,
                           reads=[kT_r[c][ktt], qT_r[c]], writes=[psZ_r])

                    def S2(n):
                        c, hh, i, kb, hd, po, ksl, ktt, diag, mi = info(n)
                        e2, s3 = n % 2, n % 3
                        psZ, psZ_r = psZs.pop(n)
                        act(ez[e2][:, :], psZ[:, :], AF.Exp, reads=[psZ_r], writes=[ez_r[e2]])
                        act(spb[s3][:, :], ez[e2][:, :], AF.Ln, reads=[ez_r[e2]], writes=[spb_r[s3]], bias=1.0)
                        if diag:
                            vtt(spb[s3][:, :], spb[s3][:, :], sbmask[mi], ALU.mult,
                                reads=[spb_r[s3], cb_r], writes=[spb_r[s3]])
                        if kb > 0:
                            if i == 0:
                                vcopy(S32[:, :], spb[s3][:, :], reads=[spb_r[s3]], writes=[S32_r])
                            else:
                                vtt(S32[:, :], S32[:, :], spb[s3][:, :], ALU.add,
                                    reads=[S32_r, spb_r[s3]], writes=[S32_r])
                            vcopy(Sbf[(n + 1) % 3][:, :], S32[:, :], reads=[S32_r], writes=[Sbf_r[(n + 1) % 3]])

                    def S3(n):
                        c, hh, i, kb, hd, po, ksl, ktt, diag, mi = info(n)
                        s3 = n % 3
                        psL, psL_r = psum()
                        psLs[n] = (psL, psL_r)
                        mm(psL[:, :], kT[po, c, ksl], qT[po, c, :], True, False,
                           reads=[kT_r[c][ktt], qT_r[c]], writes=[psL_r])
                        mm(psL[:, :], negtri_b, spb[s3][:, :], False, i == 0,
                           reads=[cb_r, spb_r[s3]], writes=[psL_r])
                        if i > 0:
                            mm(psL[:, :], negones_b, Sbf[n % 3][:, :], False, True,
                               reads=[cb_r, Sbf_r[n % 3]], writes=[psL_r])

                    def S4(n):
                        c, hh, i, kb, hd, po, ksl, ktt, diag, mi = info(n)
                        e2 = n % 2
                        if hh == 0 and i == 0:
                            psO_cur[c] = psum(hold=True)
                        psO, psO_r = psO_cur[c]
                        psL, psL_r = psLs.pop(n)
                        act(wTt[e2][:, :], psL[:, :], AF.Exp, reads=[psL_r], writes=[wT_r[e2]])
                        if diag:
                            vtt(wTt[e2][:, :], wTt[e2][:, :], sbmask[mi], ALU.mult,
                                reads=[wT_r[e2], cb_r], writes=[wT_r[e2]])
                        mm(psO[po, :], vtok[:, kb, hd * 64:(hd + 1) * 64], wTt[e2][:, :], i == 0, kb == 0,
                           reads=[vtok_r[kb], wT_r[e2]], writes=[psO_r])
                        if hh == 1 and kb == 0:
                            act(ysb[:, c, :], psO[:, :], AF.Copy, reads=[psO_r], writes=[ysb_r[c]])
                            psum_release(psO)

                    NI = len(items)
                    S1(0)
                    for a in range(NI + 2):
                        if a + 1 < NI:
                            S1(a + 1)
                        if a < NI:
                            S2(a)
                        if 0 <= a - 1 < NI:
                            S3(a - 1)
                        if 0 <= a - 2 < NI:
                            S4(a - 2)
                K.barrier()
                mark('P4 l%d t%d' % (l, tt))
                ycat = [(zs[:, c, :], zs_r[c]) for c in range(4)] + [(ysb[:, c, :], ysb_r[c]) for c in range(2)] + \
                       [(ypl[:, c, :], ypl_r[c]) for c in range(2)]
                if debug and l == 0:
                    dby_v = dby_d.rearrange("(c p) t -> p c t", p=128)
                    for kc in range(8):
                        K.dma("pool", dby_v[:, kc, tsl], ycat[kc][0], reads=[ycat[kc][1]], writes=[dbg_res])
                for oc in range(8):
                    wa, wa_r = SA.get()
                    ps, ps_r = psum()
                    for kc in range(8):
                        mm(ps[:, :], wa[:, kc, :], ycat[kc][0], kc == 0, kc == 7,
                           reads=[wa_r, ycat[kc][1]], writes=[ps_r])
                    vtt(xT[:, oc, tsl], xT[:, oc, tsl], ps[:, :], ALU.add,
                        reads=[xT_r[oc][tt], ps_r], writes=[xT_r[oc][tt]])
                if debug and l == 0:
                    dbx_v = dbx_d.rearrange("(c p) t -> p c t", p=128)
                    for oc in range(8):
                        K.dma("sp", dbx_v[:, oc, tsl], xT[:, oc, tsl], reads=[xT_r[oc][tt]], writes=[dbg_res])
                mark('P5 l%d t%d' % (l, tt))
                with ExitStack() as ph:
                    tb = lambda name, shape, dt: ph.enter_context(nc.sbuf_tensor(uname(name), shape, dt))
                    sqb = tb("sqb3", [128, 8, TT], BF16); sqb_r = R(8)
                    lnv = tb("lnv3", [128, TT], F32); lnv_r = R()
                    rstd = tb("rstd3", [128, TT], F32); rstd_r = R()
                    aT = tb("aT", [128, NFC, TT], BF16); aT_r = R(NFC)
                    sg = [tb("sg%d" % i, [128, TT], F32) for i in range(2)]; sg_r = R(2)
                    rmsnorm_tile([xT[:, c, tsl] for c in range(8)], [xT_r[c][tt] for c in range(8)], 8,
                                 [pr[:, P_N2W + c:P_N2W + c + 1] for c in range(8)], pr_r,
                                 lambda c: hT[:, c, :], hT_r, D, (sqb, sqb_r, lnv, lnv_r, rstd, rstd_r))
                    for fc in range(NFC):
                        wg, wg_r = SA.get()
                        wu, wu_r = SA.get()
                        psG, psG_r = psum()
                        psU, psU_r = psum()
                        for kc in range(8):
                            mm(psG[:, :], wg[:, kc, :], hT[:, kc, :], kc == 0, kc == 7, reads=[wg_r, hT_r[kc]], writes=[psG_r])
                        for kc in range(8):
                            mm(psU[:, :], wu[:, kc, :], hT[:, kc, :], kc == 0, kc == 7, reads=[wu_r, hT_r[kc]], writes=[psU_r])
                        b2 = fc % 2
                        act(sg[b2][:, :], psG[:, :], AF.Silu, reads=[psG_r], writes=[sg_r[b2]])
                        vtt(aT[:, fc, :], sg[b2][:, :], psU[:, :], ALU.mult, reads=[sg_r[b2], psU_r], writes=[aT_r[fc]])
                    for oc in range(8):
                        wd, wd_r = SD.get()
                        ps, ps_r = psum()
                        for fc in range(NFC):
                            mm(ps[:, :], wd[:, fc, :], aT[:, fc, :], fc == 0, fc == NFC - 1,
                               reads=[wd_r, aT_r[fc]], writes=[ps_r])
                        vtt(xT[:, oc, tsl], xT[:, oc, tsl], ps[:, :], ALU.add,
                            reads=[xT_r[oc][tt], ps_r], writes=[xT_r[oc][tt]])
                    if l == depth - 1:
                        ost = [tb("ost%d" % i, [128, 4, TT], F32) for i in range(2)]; ost_r = [R(4), R(4)]
                        out_v = out_d.rearrange("(c p) t -> p c t", p=128)
                        rmsnorm_tile([xT[:, c, tsl] for c in range(8)], [xT_r[c][tt] for c in range(8)], 8,
                                     [fnw[:, c:c + 1] for c in range(8)], fnw_r,
                                     lambda c: ost[c // 4][:, c % 4, :], ost_r[0] + ost_r[1], D,
                                     (sqb, sqb_r, lnv, lnv_r, rstd, rstd_r))
                        for c in range(8):
                            tok = K.dma("sp", out_v[:, c, tsl], ost[c // 4][:, c % 4, :],
                                        reads=[ost_r[c // 4][c % 4]], writes=[out_res[c * NTT + tt]])
                K.barrier()
        mark("END")
        K.final_wait("sp", out_res + [dbg_res])
        nc._marks = marks
    return nc


def _tile_k(w):
    K_, N_ = w.shape
    return np.ascontiguousarray(w.reshape(K_ // 128, 128, N_).transpose(1, 0, 2)).reshape(128, -1)


def _consts():
    j = np.arange(128)[:, None]
    l = np.arange(128)[None, :]
    cf = np.zeros((128, CF_N), np.float32)
    cf[:, CF_IDENT:CF_IDENT + 128] = (j == l)
    cf[:, CF_ONES:CF_ONES + 128] = 1.0
    cf[:, CF_TRI:CF_TRI + 128] = (j <= l)
    cf[:, CF_RCFIX:CF_RCFIX + 16] = 1.0 / (np.arange(16)[None, :] + 1.0)
    cb = np.zeros((128, CB_N), np.float32)
    cb[:, CB_IDENT:CB_IDENT + 128] = (j == l)
    cb[:, CB_ONES:CB_ONES + 128] = 1.0
    cb[:, CB_NEGTRI:CB_NEGTRI + 128] = -1.0 * (j >= l)
    cb[:, CB_NEGONES:CB_NEGONES + 128] = -1.0
    t = np.arange(512)[None, :]
    for i in range(4):
        cb[:, CB_MASK + i * 512:CB_MASK + (i + 1) * 512] = ((128 * i + j) < t)
    return cf, cb


def prep_weights(inp, depth):
    f = lambda a: np.asarray(a, dtype=np.float32)
    w_in, w_out = f(inp["w_in"]), f(inp["w_out"])
    w_gate, w_up, w_down = f(inp["w_gate"]), f(inp["w_up"]), f(inp["w_down"])
    fm_cols = [c * 128 for c in range(4)] + [512 + c * 128 for c in range(8)] + [1544, 1672, 1800, 1928, 2312, 2440]
    w_fm = np.empty((depth, 18, 128, 1024), np.float32)
    w_tm = np.empty((depth, 128, 8 * 264), np.float32)
    w_o = np.empty((depth, 8, 128, 1024), np.float32)
    w_gu = np.empty((depth, NFC, 2, 128, 1024), np.float32)
    w_dn = np.empty((depth, 8, 128, NFC * 128), np.float32)
    par = np.zeros((depth, 128, NPAR), np.float32)
    pw = np.zeros((depth, 128, 256), np.float32)
    for l in range(depth):
        for i, c0 in enumerate(fm_cols):
            w_fm[l, i] = _tile_k(w_in[l][:, c0:c0 + 128])
        w_tm[l] = _tile_k(np.concatenate([w_in[l][:, 2056:2312], w_in[l][:, 1536:1544]], axis=1))
        for oc in range(8):
            w_o[l, oc] = _tile_k(w_out[l][:, oc * 128:(oc + 1) * 128])
            w_dn[l, oc] = _tile_k(w_down[l][:, oc * 128:(oc + 1) * 128])
        for fc in range(NFC):
            w_gu[l, fc, 0] = _tile_k(w_gate[l][:, fc * 128:(fc + 1) * 128])
            w_gu[l, fc, 1] = _tile_k(w_up[l][:, fc * 128:(fc + 1) * 128])
        cm = lambda v: np.asarray(v, np.float32).reshape(-1, 128).T
        par[l, :, P_N1W:P_N1W + 8] = cm(inp["norm1_w"][l])
        par[l, :, P_N2W:P_N2W + 8] = cm(inp["norm2_w"][l])
        cw = f(inp["conv_w"][l])
        for i in range(4):
            par[l, :, P_CW + i:P_CW + 32:4] = cm(cw[i])
        par[l, :, P_CB:P_CB + 8] = cm(inp["conv_b"][l])
        par[l, :, P_SNW:P_SNW + 4] = cm(inp["ssd_norm_w"][l])
        par[l, :, P_PB:P_PB + 2] = cm(f(inp["pool_b"][l]).reshape(-1))
        par[l, :, P_PS:P_PS + 2] = cm(inp["pool_scale"][l])
        par[l, :, P_DTB:P_DTB + 8] = f(inp["dt_bias"][l])[None, :]
        par[l, :, P_ALOG:P_ALOG + 8] = f(inp["a_log"][l])[None, :]
        par[l, :, P_DSK:P_DSK + 8] = f(inp["d_skip"][l])[None, :]
        pwl = f(inp["pool_w"][l])
        for i in range(2):
            for hh in range(2):
                pw[l, hh * 64:(hh + 1) * 64, i * 128 + hh * 64:i * 128 + (hh + 1) * 64] = pwl[2 * i + hh]
    cf, cb = _consts()
    fnw = np.ascontiguousarray(f(inp["final_norm_w"]).reshape(8, 128).T)
    return dict(w_fm=w_fm, w_tm=w_tm, w_o=w_o, w_gu=w_gu, w_dn=w_dn, par=par, pw=pw, fnw=fnw, cf=cf, cb=cb)


_PROG = {}


def run(inputs, depth=DEPTH, cores=NCORES, debug=False):
    if (depth, debug) not in _PROG:
        _PROG[(depth, debug)] = build_program(depth, debug)
    nc = _PROG[(depth, debug)]
    wts = prep_weights(inputs, depth)
    x = np.asarray(inputs["x"], dtype=np.float32)
    in_maps = []
    for b in range(cores):
        m = dict(wts)
        m["xT"] = np.ascontiguousarray(x[b].T)
        in_maps.append(m)
    res = run_bass_kernel_spmd(nc, in_maps, core_ids=list(range(cores)))
    out = np.stack([np.ascontiguousarray(r["outT"].T) for r in res.results], axis=0)
    if debug:
        return out.astype(np.float32), [dict(r) for r in res.results]
    return out.astype(np.float32)


def kernel(**inputs):
    return run(inputs, DEPTH, NCORES)
```
